# Optimizing a Trainium2 kernel written in Bass

```python
import math
import jax, jax.numpy as jnp
from jax import lax
import numpy as np

D_MODEL = 1024
BATCH = 2
SEQ = 8192
DEPTH = 2

GRID_W = 64
CTX_LEN = 256
HEAD_DIM = 64
N_Q_HEADS = 8
N_KV_HEADS = 2
Q_PER_KV = N_Q_HEADS // N_KV_HEADS
ATTN_WIDTH = N_Q_HEADS * HEAD_DIM
KV_WIDTH = N_KV_HEADS * HEAD_DIM
WINDOW = 128
BLOCK = 128
ROPE_BASE = 10000.0
POOL_WINDOWS = (2, 4, 8, 16)
POOL_GROUP = 64
POOL_WIDTH = POOL_GROUP * len(POOL_WINDOWS)
CONV_WIDTH = 256
CONV_K = 31
N_BRANCH = 3
SPLIT_POOL = POOL_WIDTH
SPLIT_Q = SPLIT_POOL + ATTN_WIDTH
SPLIT_K = SPLIT_Q + KV_WIDTH
SPLIT_V = SPLIT_K + KV_WIDTH
SPLIT_CONV = SPLIT_V + 2 * CONV_WIDTH
IN_WIDTH = SPLIT_CONV + N_BRANCH * D_MODEL
D_FF = 4 * D_MODEL
ALPHA = (2 * DEPTH) ** 0.25
BETA = (8 * DEPTH) ** -0.25
LN_EPS = 1e-5
NEG = -1e30

kernel_name = "hybrid_pool_swa_conformer_deepnorm_dit"


def _layer_norm(x):
    xf = x.astype(jnp.float32)
    mu = jnp.mean(xf, axis=-1, keepdims=True)
    var = jnp.mean(jnp.square(xf - mu), axis=-1, keepdims=True)
    return ((xf - mu) * lax.rsqrt(var + LN_EPS)).astype(x.dtype)


def _modulate(x, shift, scale):
    return _layer_norm(x) * (1 + scale) + shift


def _post_norm(x, gate, y, g, b):
    return _layer_norm(ALPHA * x + gate * y) * g + b


def _pool_mixer(u, pool_w, pool_scale):
    B, L, _ = u.shape
    uf = u.astype(jnp.float32)
    cs = jnp.concatenate([jnp.zeros((B, 1, POOL_WIDTH), jnp.float32), jnp.cumsum(uf, axis=1)], axis=1)
    t = jnp.arange(L)
    outs = []
    for g, w in enumerate(POOL_WINDOWS):
        lo = jnp.clip(t - w // 2, 0, L)
        hi = jnp.clip(t + w - w // 2, 0, L)
        csg = cs[..., g * POOL_GROUP:(g + 1) * POOL_GROUP]
        cnt = (hi - lo).astype(jnp.float32)[None, :, None]
        outs.append((csg[:, hi] - csg[:, lo]) / cnt)
    pooled = jnp.concatenate(outs, axis=-1)
    d = (pooled - uf).astype(u.dtype).reshape(B, L, len(POOL_WINDOWS), POOL_GROUP)
    y = jnp.einsum('blgc,gcd->blgd', d, pool_w).reshape(B, L, POOL_WIDTH)
    return y * pool_scale


def _axial_rope(x, row_pos, col_pos):
    half = HEAD_DIM // 2
    quarter = half // 2
    inv = ROPE_BASE ** (-jnp.arange(quarter, dtype=jnp.float32) / quarter)

    def rot(xa, pos):
        ang = pos.astype(jnp.float32)[:, None] * inv[None, :]
        cos = jnp.cos(ang)[None, :, None, :]
        sin = jnp.sin(ang)[None, :, None, :]
        x1, x2 = xa[..., :quarter], xa[..., quarter:]
        return jnp.concatenate([x1 * cos - x2 * sin, x1 * sin + x2 * cos], axis=-1)

    xf = x.astype(jnp.float32)
    out = jnp.concatenate([rot(xf[..., :half], row_pos), rot(xf[..., half:], col_pos)], axis=-1)
    return out.astype(x.dtype)


def _latent_attention(q, k, v, kc, vc, sink):
    B, L = q.shape[0], q.shape[1]
    C = kc.shape[1]
    nb = L // BLOCK
    scale = HEAD_DIM ** -0.5
    qb = q.reshape(B, nb, BLOCK, N_KV_HEADS, Q_PER_KV, HEAD_DIM)
    pad = jnp.zeros((B, BLOCK, N_KV_HEADS, HEAD_DIM), k.dtype)

    def band(t):
        tp = jnp.concatenate([pad, t, pad], axis=1).reshape(B, nb + 2, BLOCK, N_KV_HEADS, HEAD_DIM)
        return jnp.concatenate([tp[:, :-2], tp[:, 1:-1], tp[:, 2:]], axis=2)

    kb, vb = band(k), band(v)
    s_loc = jnp.einsum('bnqhgd,bnkhd->bnhgqk', qb, kb).astype(jnp.float32) * scale
    qi = jnp.arange(nb)[:, None, None] * BLOCK + jnp.arange(BLOCK)[None, :, None]
    kj = (jnp.arange(nb)[:, None, None] - 1) * BLOCK + jnp.arange(3 * BLOCK)[None, None, :]
    valid = (jnp.abs(qi - kj) <= WINDOW) & (kj >= 0) & (kj < L)
    s_loc = jnp.where(valid[None, :, None, None], s_loc, NEG)
    s_ctx = jnp.einsum('bnqhgd,bchd->bnhgqc', qb, kc).astype(jnp.float32) * scale
    s_sink = jnp.broadcast_to(
        sink.reshape(N_KV_HEADS, Q_PER_KV)[None, None, :, :, None, None].astype(jnp.float32),
        s_loc.shape[:-1] + (1,))
    p = jax.nn.softmax(jnp.concatenate([s_loc, s_ctx, s_sink], axis=-1), axis=-1)
    p_loc = p[..., :3 * BLOCK].astype(v.dtype)
    p_ctx = p[..., 3 * BLOCK:3 * BLOCK + C].astype(v.dtype)
    o = (jnp.einsum('bnhgqk,bnkhd->bnqhgd', p_loc, vb)
         + jnp.einsum('bnhgqc,bchd->bnqhgd', p_ctx, vc))
    return o.reshape(B, L, ATTN_WIDTH)


def _context_attention(qc, kc, vc, sink):
    B, C = qc.shape[0], qc.shape[1]
    scale = HEAD_DIM ** -0.5
    s = jnp.einsum('bqhgd,bkhd->bhgqk', qc, kc).astype(jnp.float32) * scale
    s_sink = jnp.broadcast_to(
        sink.reshape(N_KV_HEADS, Q_PER_KV)[None, :, :, None, None].astype(jnp.float32),
        s.shape[:-1] + (1,))
    p = jax.nn.softmax(jnp.concatenate([s, s_sink], axis=-1), axis=-1)
    o = jnp.einsum('bhgqk,bkhd->bqhgd', p[..., :C].astype(vc.dtype), vc)
    return o.reshape(B, C, ATTN_WIDTH)


def _conv_module(u, conv_w, conv_b, ln_g, ln_b, w_proj):
    a, b = jnp.split(u, 2, axis=-1)
    glu = a * jax.nn.sigmoid(b)
    y = lax.conv_general_dilated(glu, conv_w[:, None, :], window_strides=(1,),
                                 padding=[(CONV_K // 2, CONV_K // 2)],
                                 dimension_numbers=('NWC', 'WIO', 'NWC'),
                                 feature_group_count=CONV_WIDTH) + conv_b
    y = jax.nn.silu(_layer_norm(y) * ln_g + ln_b)
    return y @ w_proj


def _split_in(z):
    return jnp.split(z, [SPLIT_POOL, SPLIT_Q, SPLIT_K, SPLIT_V, SPLIT_CONV], axis=-1)


def _merge(gates, o_attn, u_pool, u_conv, pool_w, pool_scale, w_pool_out, w_attn_out,
           conv_w, conv_b, conv_ln_g, conv_ln_b, w_conv_out, w_out):
    y_pool = _pool_mixer(u_pool, pool_w, pool_scale) @ w_pool_out
    y_attn = o_attn @ w_attn_out
    y_conv = _conv_module(u_conv, conv_w, conv_b, conv_ln_g, conv_ln_b, w_conv_out)
    g = jax.nn.sigmoid(gates.astype(jnp.float32)).astype(gates.dtype)
    g_pool, g_attn, g_conv = jnp.split(g, N_BRANCH, axis=-1)
    return (g_pool * y_pool + g_attn * y_attn + g_conv * y_conv) @ w_out


def _mlp(h, w1, b1, w2, b2):
    return jnp.square(jax.nn.relu(h @ w1 + b1)) @ w2 + b2


def setup_inputs(seed: int = 0) -> dict:
    key = jax.random.key(seed)
    ks = jax.random.split(key, 32)
    f = jnp.float32
    D = D_MODEL

    def nrm(k, shape, s):
        return jax.random.normal(k, shape, f) * s

    return {
        "x": nrm(ks[0], (BATCH, SEQ, D), 1.0),
        "c": nrm(ks[1], (BATCH, D), 1.0),
        "ctx": nrm(ks[2], (BATCH, CTX_LEN, D), 1.0),
        "c_ctx": nrm(ks[3], (D,), 1.0),
        "w_mod": nrm(ks[4], (DEPTH, D, 6 * D), 0.5 * D ** -0.5),
        "b_mod": nrm(ks[5], (DEPTH, 6 * D), 0.02),
        "w_in": nrm(ks[6], (DEPTH, D, IN_WIDTH), D ** -0.5),
        "b_in": nrm(ks[7], (DEPTH, IN_WIDTH), 0.02),
        "pool_w": nrm(ks[8], (DEPTH, len(POOL_WINDOWS), POOL_GROUP, POOL_GROUP), POOL_GROUP ** -0.5),
        "pool_scale": 1.0 + nrm(ks[9], (DEPTH, POOL_WIDTH), 0.02),
        "w_pool_out": nrm(ks[10], (DEPTH, POOL_WIDTH, D), POOL_WIDTH ** -0.5),
        "attn_sink": nrm(ks[11], (DEPTH, N_Q_HEADS), 0.5),
        "w_attn_out": nrm(ks[12], (DEPTH, ATTN_WIDTH, D), ATTN_WIDTH ** -0.5),
        "conv_w": nrm(ks[13], (DEPTH, CONV_K, CONV_WIDTH), CONV_K ** -0.5),
        "conv_b": nrm(ks[14], (DEPTH, CONV_WIDTH), 0.02),
        "conv_ln_g": 1.0 + nrm(ks[15], (DEPTH, CONV_WIDTH), 0.02),
        "conv_ln_b": nrm(ks[16], (DEPTH, CONV_WIDTH), 0.02),
        "w_conv_out": nrm(ks[17], (DEPTH, CONV_WIDTH, D), CONV_WIDTH ** -0.5),
        "w_out": nrm(ks[18], (DEPTH, D, D), BETA * D ** -0.5),
        "ln1_g": 1.0 + nrm(ks[19], (DEPTH, D), 0.02),
        "ln1_b": nrm(ks[20], (DEPTH, D), 0.02),
        "w_mlp1": nrm(ks[21], (DEPTH, D, D_FF), D ** -0.5),
        "b_mlp1": nrm(ks[22], (DEPTH, D_FF), 0.02),
        "w_mlp2": nrm(ks[23], (DEPTH, D_FF, D), BETA * D_FF ** -0.5),
        "b_mlp2": nrm(ks[24], (DEPTH, D), 0.02),
        "ln2_g": 1.0 + nrm(ks[25], (DEPTH, D), 0.02),
        "ln2_b": nrm(ks[26], (DEPTH, D), 0.02),
    }


def reference(x, c, ctx, c_ctx, w_mod, b_mod, w_in, b_in, pool_w, pool_scale, w_pool_out,
              attn_sink, w_attn_out, conv_w, conv_b, conv_ln_g, conv_ln_b, w_conv_out, w_out,
              ln1_g, ln1_b, w_mlp1, b_mlp1, w_mlp2, b_mlp2, ln2_g, ln2_b):
    B, L, _ = x.shape
    C = ctx.shape[1]
    rows = L // GRID_W
    row_pos = jnp.repeat(jnp.arange(rows, dtype=jnp.int32), GRID_W)
    col_pos = jnp.tile(jnp.arange(GRID_W, dtype=jnp.int32), rows)
    silu_c = jax.nn.silu(c)
    silu_cc = jax.nn.silu(c_ctx)
    xl, xc = x, ctx
    for i in range(DEPTH):
        last = i == DEPTH - 1
        mod_l = (silu_c @ w_mod[i] + b_mod[i])[:, None, :]
        mod_c = silu_cc @ w_mod[i] + b_mod[i]
        sh1_l, sc1_l, g1_l, sh2_l, sc2_l, g2_l = jnp.split(mod_l, 6, axis=-1)
        sh1_c, sc1_c, g1_c, sh2_c, sc2_c, g2_c = jnp.split(mod_c, 6, axis=-1)
        branch_params = (pool_w[i], pool_scale[i], w_pool_out[i], w_attn_out[i], conv_w[i], conv_b[i],
                         conv_ln_g[i], conv_ln_b[i], w_conv_out[i], w_out[i])

        hl = _modulate(xl, sh1_l, sc1_l)
        hc = _modulate(xc, sh1_c, sc1_c)
        ul_pool, ul_q, ul_k, ul_v, ul_conv, ul_gate = _split_in(hl @ w_in[i] + b_in[i])
        uc_pool, uc_q, uc_k, uc_v, uc_conv, uc_gate = _split_in(hc @ w_in[i] + b_in[i])
        kc = uc_k.reshape(B, C, N_KV_HEADS, HEAD_DIM)
        vc = uc_v.reshape(B, C, N_KV_HEADS, HEAD_DIM)
        ql = _axial_rope(ul_q.reshape(B, L, N_Q_HEADS, HEAD_DIM), row_pos, col_pos)
        ql = ql.reshape(B, L, N_KV_HEADS, Q_PER_KV, HEAD_DIM)
        kl = _axial_rope(ul_k.reshape(B, L, N_KV_HEADS, HEAD_DIM), row_pos, col_pos)
        vl = ul_v.reshape(B, L, N_KV_HEADS, HEAD_DIM)
        ol = _latent_attention(ql, kl, vl, kc, vc, attn_sink[i])
        yl = _merge(ul_gate, ol, ul_pool, ul_conv, *branch_params)
        xl_new = _post_norm(xl, g1_l, yl, ln1_g[i], ln1_b[i])
        if not last:
            oc = _context_attention(uc_q.reshape(B, C, N_KV_HEADS, Q_PER_KV, HEAD_DIM), kc, vc, attn_sink[i])
            yc = _merge(uc_gate, oc, uc_pool, uc_conv, *branch_params)
            xc = _post_norm(xc, g1_c, yc, ln1_g[i], ln1_b[i])
        xl = xl_new

        yl = _mlp(_modulate(xl, sh2_l, sc2_l), w_mlp1[i], b_mlp1[i], w_mlp2[i], b_mlp2[i])
        xl = _post_norm(xl, g2_l, yl, ln2_g[i], ln2_b[i])
        if not last:
            yc = _mlp(_modulate(xc, sh2_c, sc2_c), w_mlp1[i], b_mlp1[i], w_mlp2[i], b_mlp2[i])
            xc = _post_norm(xc, g2_c, yc, ln2_g[i], ln2_b[i])
    return xl
```

```python
import math, os
_SUB = int(os.environ.get('SUB', '99'))
_SUB2 = int(os.environ.get('SUB2', '99'))
import numpy as np
from contextlib import ExitStack
import concourse.bass as bass
import concourse.mybir as mybir
from concourse.bass_utils import run_bass_kernel_spmd
F32 = mybir.dt.float32
BF16 = mybir.dt.bfloat16
ALU = mybir.AluOpType
AF = mybir.ActivationFunctionType
AX = mybir.AxisListType

ENGS = ("pe", "act", "dve", "pool", "sp")
GRAN = 128
SEM_CAP = 30000
N_DMA_SEM = 40
N_SW_SEM = 8
SAME_ENG_WINDOW = 6


class Buf:
    def __init__(self, tensor, tname, off, n, parts=128):
        self.t = tensor
        self.tname = tname
        self.off = off
        self.n = n
        self.parts = parts

    def ap(self, lo=0, hi=None, p0=0, p1=None):
        hi = self.n if hi is None else hi
        p1 = self.parts if p1 is None else p1
        return self.t[p0:p1, self.off + lo:self.off + hi]

    def v(self, pattern=None, lo=0, hi=None, p0=0, p1=None, **kw):
        a = self.ap(lo, hi, p0, p1)
        if pattern:
            a = a.rearrange(pattern, **kw)
        return a

    def keys(self, lo=0, hi=None):
        hi = self.n if hi is None else hi
        g0 = (self.off + lo) // GRAN
        g1 = (self.off + hi - 1) // GRAN
        return [(self.tname, g) for g in range(g0, g1 + 1)]


def _keys(items):
    out = []
    for it in items:
        if isinstance(it, Buf):
            out.extend(it.keys())
        elif isinstance(it, tuple) and len(it) == 3 and isinstance(it[0], Buf):
            out.extend(it[0].keys(it[1], it[2]))
        else:
            out.append(it)
    return out


class Op:
    __slots__ = ("eng", "fn", "deps", "idx", "is_dma", "dsem", "dval", "sig",
                 "rank", "waits", "gidx")


class Sched:
    def __init__(self):
        self.ops = {e: [] for e in ENGS}
        self.state = {}
        self.ndma = 0
        self.nsw = 0
        self.dma_last = [None] * N_DMA_SEM
        self.dma_cnt = [0] * N_DMA_SEM
        self.gcount = 0
        self.out_dmas = []

    def add(self, eng, fn, r=(), w=(), dma=False, out=False):
        o = Op()
        o.eng = eng
        o.fn = fn
        o.is_dma = dma
        o.idx = len(self.ops[eng])
        o.deps = set()
        o.sig = False
        o.gidx = self.gcount
        self.gcount += 1
        rk = _keys(r)
        wk = _keys(w)
        st = self.state
        for k in rk:
            s = st.get(k)
            if s is not None and s[0] is not None:
                o.deps.add(s[0])
        for k in wk:
            s = st.get(k)
            if s is not None:
                if s[0] is not None:
                    o.deps.add(s[0])
                o.deps.update(s[1].values())
        for k in rk:
            s = st.get(k)
            if s is None:
                s = st[k] = [None, {}]
            s[1][(eng, o.idx) if dma else eng] = o
        for k in wk:
            st[k] = [o, {}]
        if dma:
            if eng == "pool":
                si = self.nsw % N_SW_SEM
                self.nsw += 1
            else:
                si = N_SW_SEM + self.ndma % (N_DMA_SEM - N_SW_SEM)
                self.ndma += 1
            prev = self.dma_last[si]
            if prev is not None:
                o.deps.add(prev)
            self.dma_cnt[si] += 1
            o.dsem = si
            o.dval = 16 * self.dma_cnt[si]
            self.dma_last[si] = o
            if out:
                self.out_dmas.append(o)
        o.deps.discard(o)
        self.ops[eng].append(o)
        return o

    def finalize_waits(self):
        fin = Op()
        fin.eng = "sp"
        fin.fn = None
        fin.is_dma = False
        fin.idx = len(self.ops["sp"])
        fin.deps = set(self.out_dmas) | set(d for d in self.dma_last if d is not None)
        fin.sig = False
        fin.gidx = self.gcount
        self.ops["sp"].append(fin)
        for e in ENGS:
            for o in self.ops[e]:
                need = []
                for d in o.deps:
                    if d.is_dma:
                        need.append(d)
                    elif d.eng == o.eng and not o.is_dma:
                        if e == "pe":
                            continue
                        need.append(d)
                    else:
                        need.append(d)
                o.deps = need
                for d in need:
                    if not d.is_dma:
                        d.sig = True
        self.nchain = {}
        for e in ENGS:
            r = 0
            for o in self.ops[e]:
                if o.sig and not o.is_dma:
                    o.rank = r
                    r += 1
            self.nchain[e] = (r + SEM_CAP - 1) // SEM_CAP
        for e in ENGS:
            seen = {}
            for o in self.ops[e]:
                ws = {}
                for d in o.deps:
                    if d.is_dma:
                        key = ("dma", d.dsem)
                        val = d.dval
                    else:
                        key = (d.eng, d.rank // SEM_CAP)
                        val = d.rank % SEM_CAP + 1
                    if seen.get(key, 0) >= val:
                        continue
                    if ws.get(key, 0) < val:
                        ws[key] = val
                for k, v in ws.items():
                    seen[k] = v
                o.waits = list(ws.items())

    def emit(self, nc, stack):
        self.finalize_waits()
        sems = {}
        for e in ENGS:
            for c in range(self.nchain[e]):
                sems[(e, c)] = stack.enter_context(nc.semaphore(f"s_{e}_{c}"))
        for i in range(N_DMA_SEM):
            sems[("dma", i)] = stack.enter_context(nc.semaphore(f"s_dma_{i}"))
        block = stack.enter_context(nc.Block())

        def body(e):
            def f(eng):
                for o in self.ops[e]:
                    for k, v in o.waits:
                        eng.wait_ge(sems[k], v)
                    if o.fn is None:
                        continue
                    ins = o.fn(eng)
                    if o.is_dma:
                        ins.then_inc(sems[("dma", o.dsem)], 16)
                    elif o.sig:
                        ins.then_inc(sems[(e, o.rank // SEM_CAP)], 1)
            return f

        block.tensor(body("pe"))
        block.scalar(body("act"))
        block.vector(body("dve"))
        block.gpsimd(body("pool"))
        block.sync(body("sp"))

D = 1024
L = 8192
CTX = 256
DEPTH = 2
NT = 22
NSL = 20
T = 256
ALPHA = (2 * DEPTH) ** 0.25
EPS = 1e-5
NFM = 192
NROW = 128 + 5 * 1024
DFF = 4096

FM_BIN = 0
FM_B1 = 36
FM_BMOD = 68
FM_PS = 116
FM_CB = 118
FM_CG = 120
FM_CBETA = 122
FM_CW = 124
FM_SINK = 186

A_S1W = 0
A_S3W = 8192
A_WPO = 36864
A_WCO = 38912
A_WOUT = 40960
A_WAO = 49152
A_PBD = 53248
A_MODST = 53760
A_MT = 55808
A_XT2 = 57856
A_TT = 59904
A_HT2 = 61952
A_PT = 64000
A_W1 = 0
A_W2 = 32768


def build_program(debug=False, nstages=10**9):
    _bud = [nstages]
    def _go():
        _bud[0] -= 1
        return _bud[0] >= 0
    nc = bass.Bass("TRN2", target_bir_lowering=False)
    dt_in = lambda name, shape: nc.dram_tensor(name, shape, F32, kind="ExternalInput").ap()
    xin = dt_in("xin", [NT * 128, D])
    tab = dt_in("tab", [128, 5, NT * 128])
    kbias_d = dt_in("kbias", [128, 32])
    cvec_d = dt_in("cvec", [128, 16])
    const_d = dt_in("consts", [128, 4, 128])
    fm_d = dt_in("fm", [DEPTH, 128, NFM])
    rowv_d = dt_in("rowv", [DEPTH, 1, NROW])
    w_mod = dt_in("w_mod", [DEPTH, D, 6 * D])
    w_in = dt_in("w_in_r", [DEPTH, D, 36 * 128])
    pbd_d = dt_in("pbd", [DEPTH, 128, 256])
    wpo_d = dt_in("w_pool_out", [DEPTH, 256, D])
    wao_d = dt_in("wao_r", [DEPTH, 128, 4 * D])
    wco_d = dt_in("w_conv_out", [DEPTH, 256, D])
    wout_d = dt_in("w_out", [DEPTH, D, D])
    w1_d = dt_in("w_mlp1", [DEPTH, D, DFF])
    w2_d = dt_in("w_mlp2", [DEPTH, DFF, D])
    out_d = nc.dram_tensor("out", [16 * 128, D], F32, kind="ExternalOutput").ap()
    scr = lambda name, shape, dt=F32: nc.dram_tensor(name, shape, dt, kind="Internal").ap()
    x1D = scr("x1D", [NT * 128, D])
    x2D = scr("x2D", [NT * 128, D])
    hTD = scr("hTD", [128, 8, NT * 128], BF16)
    WUP = NT * 128 + 96
    upD = scr("upD", [128, 2, WUP])
    gluD = scr("gluD", [128, 2, WUP])
    dbg = {}
    if debug:
        dbg["x1"] = nc.dram_tensor("dbg_x1", [NT * 128, D], F32, kind="ExternalOutput").ap()
        dbg["x2"] = nc.dram_tensor("dbg_x2", [NT * 128, D], F32, kind="ExternalOutput").ap()

    S = Sched()
    st = ExitStack()
    with st:
        arena_t = st.enter_context(nc.sbuf_tensor("arena", [128, 65536], BF16))
        NPF = 2048
        pf_t = st.enter_context(nc.sbuf_tensor("pf", [128, NPF], F32))
        NKV = 10 * 128
        NPB = 512 + 2 * NKV
        pb_t = st.enter_context(nc.sbuf_tensor("pb", [128, NPB], BF16))
        bc_t = st.enter_context(nc.sbuf_tensor("bc", [128, 4096], F32))
        NSC = 22528
        sc_t = st.enter_context(nc.sbuf_tensor("scr", [128, NSC], BF16))
        arena_f = arena_t[:, :].bitcast(F32)
        sc_f = sc_t[:, :].bitcast(F32)
        pT = [st.enter_context(nc.psum_tensor(f"pT{i}", [128, 1024], BF16)) for i in range(2)]
        pF = [st.enter_context(nc.psum_tensor(f"pF{i}", [128, 512], F32)) for i in range(6)]

        class FBuf(Buf):
            def __init__(self, fview, tname, off_bf, n_f32):
                Buf.__init__(self, fview, tname, off_bf // 2, n_f32)
                self.off_bf = off_bf
            def keys(self, lo=0, hi=None):
                hi = self.n if hi is None else hi
                g0 = (self.off_bf + 2 * lo) // GRAN
                g1 = (self.off_bf + 2 * hi - 1) // GRAN
                return [(self.tname, g) for g in range(g0, g1 + 1)]

        def AB(off, n):
            return Buf(arena_t, "arena", off, n)

        cur = {"pf": 0, "pb": 0, "scr": 0}
        tens = {"pf": pf_t, "pb": pb_t, "scr": sc_t}
        lim = {"pf": NPF, "pb": NPB, "scr": NSC}

        def alloc(region, n, al=GRAN):
            n_al = (n + al - 1) // al * al
            off = cur[region]
            cur[region] += n_al
            assert cur[region] <= lim[region], (region, cur[region])
            return Buf(tens[region], region, off, n)

        def allocf(n):
            n_al = (2 * n + GRAN - 1) // GRAN * GRAN
            off = cur["scr"]
            cur["scr"] += n_al
            assert cur["scr"] <= NSC, cur["scr"]
            return FBuf(sc_f, "scr", off, n)

        A32 = 32
        identf = alloc("pf", 128); permf = alloc("pf", 128); onesf = alloc("pf", 128); onesLN = alloc("pf", 128)
        fm = [alloc("pf", NFM, A32) for _ in range(DEPTH)]
        modT = [alloc("pf", 96, A32) for _ in range(DEPTH)]
        kbias = alloc("pf", 32, A32); silc = alloc("pf", 16, A32); bvb = alloc("pf", 128, A32); esink = alloc("pf", 8, A32)
        stat = [alloc("pf", 16, A32) for _ in range(4)]
        diag = alloc("pf", 128, A32)
        epsb = alloc("pf", 8, A32)
        identb = alloc("pb", 128); mprev = alloc("pb", 128); mnext = alloc("pb", 128); onesb = alloc("pb", 128)
        kT = alloc("pb", NKV); Vb = alloc("pb", NKV)
        def kvcol(t):
            return (t % 8) * 128 if t < NSL else (8 + t - NSL) * 128
        BC = [Buf(bc_t, "bc", i * 1024, 1024) for i in range(4)]
        a_tab = allocf(768); a_t1 = allocf(256); a_t2 = allocf(256); a_sgb = allocf(256)
        a_upst = allocf(512); a_glust = allocf(512)
        f_tab = allocf(1024); f_upW = allocf(2 * 272); f_gluW = allocf(2 * 288)
        f_reg = allocf(2304)
        a_xn = alloc("scr", 1024); a_hT = alloc("scr", 2048); f_qT = alloc("scr", 1024); f_oT = alloc("scr", 1024)
        f_dT = alloc("scr", 512); f_ypT = alloc("scr", 512); f_sT = alloc("scr", 512)
        R = f_reg
        def sub(b, lo, n):
            return FBuf(sc_f, "scr", b.off_bf + 2 * lo, n)
        a_cst = sub(R, 0, 512)
        q_f = sub(R, 0, 1024); q_t1 = sub(R, 1024, 256); q_t2 = sub(R, 1280, 256)
        at_den = sub(R, 1536, 512)
        p_a = sub(R, 0, 544); p_b = sub(R, 640, 544); p_tmp = sub(R, 1280, 256)
        c_acc0 = sub(R, 0, 512); c_acc1 = sub(R, 512, 512); c_mean = sub(R, 1024, 256); c_var = sub(R, 1280, 256)
        c_z = sub(R, 1536, 512)
        m_gs = sub(R, 0, 768); m_1 = sub(R, 768, 256); m_2 = sub(R, 1024, 256)
        f_mT = AB(A_MT, 2048); f_hT2 = AB(A_HT2, 2048); f_PT = [AB(A_PT + i * 512, 512) for i in range(3)]
        f_xt2 = FBuf(arena_f, "arena", A_XT2, 1024); f_tt = FBuf(arena_f, "arena", A_TT, 1024)
        modst = FBuf(arena_f, "arena", A_MODST, 1024)
        a_xt = BC[3]
        cur["scr"] = 0
        b_xt = allocf(1024); b_pre = allocf(1024); b_tt = allocf(1024)
        b_r = [allocf(256) for _ in range(3)]
        b_xn = alloc("scr", 1024); b_hT = alloc("scr", 2048); b_fT = alloc("scr", 8192)
        W1 = AB(A_W1, 32768); W2 = AB(A_W2, 32768)
        S1W = AB(A_S1W, 8192); S3W = AB(A_S3W, 28672); WPO = AB(A_WPO, 2048); WCO = AB(A_WCO, 2048)
        WOUT = AB(A_WOUT, 8192); WAO = AB(A_WAO, 4096); PBD = AB(A_PBD, 256)

        psk = lambda kind, i, h=None: ("ps", kind, i) if h is None else ("ps", kind, i, h)
        fm_rot = [0]
        FM_BANKS = [0, 1, 4, 5]
        def fm_slot():
            i = FM_BANKS[fm_rot[0] % 4]
            fm_rot[0] += 1
            return pF[i][:, 0:256], psk("F", i)
        big_rot = [0]
        def big_slot():
            i = 2 + big_rot[0] % 2
            big_rot[0] += 1
            return pF[i], psk("F", i)
        st_rot = [0]
        def stat_buf():
            st_rot[0] += 1
            return stat[st_rot[0] % 4]

        def dma(q, out, in_, r=(), w=(), is_out=False):
            S.add(q, lambda e: e.dma_start(out=out, in_=in_), r=r, w=w, dma=True, out=is_out)

        def act(out, in_, func, r, w, bias=0.0, scale=1.0):
            S.add("act", lambda e: e.activation(out=out, in_=in_, func=func, bias=bias, scale=scale), r=r, w=w)

        def tt(eng, out, in0, in1, op, r, w):
            S.add(eng, lambda e: e.tensor_tensor(out=out, in0=in0, in1=in1, op=op), r=r, w=w)

        def stt(eng, out, in0, scalar, in1, op0, op1, r, w):
            S.add(eng, lambda e: e.scalar_tensor_tensor(out=out, in0=in0, scalar=scalar, in1=in1, op0=op0, op1=op1),
                  r=r, w=w)

        def ts(eng, out, in0, s1, s2, op0, op1, r, w):
            S.add(eng, lambda e: e.tensor_scalar(out=out, in0=in0, scalar1=s1, scalar2=s2, op0=op0, op1=op1), r=r, w=w)

        def ts1(eng, out, in_, scalar, op, r, w):
            S.add(eng, lambda e: e.tensor_single_scalar(out=out, in_=in_, scalar=scalar, op=op), r=r, w=w)

        def mmK(out, pairs, r, w):
            n = len(pairs)
            def f(e):
                ins = None
                for i, (l_, r_) in enumerate(pairs):
                    ins = e.matmul(out, lhsT=l_, rhs=r_, start=(i == 0), stop=(i == n - 1))
                return ins
            S.add("pe", f, r=r, w=w)

        def layer_norm_rows(src, dst_ap, srcbuf_keys, dst_keys):
            sb_ = stat_buf()
            S.add("dve", lambda e: e.bn_stats(out=sb_.ap(0, 6), in_=src.ap(0, 512)), r=srcbuf_keys, w=[(sb_, 0, 6)])
            S.add("dve", lambda e: e.bn_stats(out=sb_.ap(6, 12), in_=src.ap(512, 1024)), r=srcbuf_keys, w=[(sb_, 6, 12)])
            S.add("dve", lambda e: e.bn_aggr(out=sb_.ap(12, 14), in_=sb_.ap(0, 12).rearrange("p (a b) -> p a b", b=3)),
                  r=[sb_], w=[sb_])
            act(sb_.ap(14, 15), sb_.ap(13, 14), AF.Sqrt, r=[sb_], w=[sb_], bias=eps_ap(), scale=1.0)
            S.add("dve", lambda e: e.reciprocal(out=sb_.ap(14, 15), in_=sb_.ap(14, 15)), r=[sb_], w=[sb_])
            ts("dve", sb_.ap(15, 16), sb_.ap(12, 13), sb_.ap(14, 15), -1.0, ALU.mult, ALU.mult, r=[sb_], w=[sb_])
            act(dst_ap, src.ap(), AF.Identity, r=srcbuf_keys + [sb_], w=dst_keys, bias=sb_.ap(15, 16), scale=sb_.ap(14, 15))

        def eps_ap():
            return epsb.ap(0, 1)

        S.add("dve", lambda e: e.memset(epsb.ap(), EPS), w=[epsb])
        S.add("dve", lambda e: e.memset(onesf.ap(), 1.0), w=[onesf])
        S.add("dve", lambda e: e.memset(onesLN.ap(), 1.0 / 256.0), w=[onesLN])
        S.add("dve", lambda e: e.memset(onesb.ap(), 1.0), w=[onesb])
        dma("sp", identf.ap(), const_d[:, 0, :], w=[identf])
        dma("sp", permf.ap(), const_d[:, 1, :], w=[permf])
        dma("pool", mprev.ap(), const_d[:, 2, :], w=[mprev])
        dma("pool", mnext.ap(), const_d[:, 3, :], w=[mnext])
        if os.environ.get('IDACT'):
            act(identb.ap(), identf.ap(), AF.Copy, r=[identf], w=[identb])
        else:
            dma("pool", identb.ap(), const_d[:, 0, :], w=[identb])
        dma("sp", kbias.ap(), kbias_d, w=[kbias])
        dma("sp", silc.ap(), cvec_d, w=[silc])
        for l in range(DEPTH):
            dma("sp", fm[l].ap(), fm_d[l], w=[fm[l]])
        act(silc.ap(), silc.ap(), AF.Silu, r=[silc], w=[silc])
        S.add("dve", lambda e: e.memset(a_cst.ap(), 0.0), w=[a_cst])
        ZC0 = 32
        CC0 = 32 + NSL * 128 + 32
        def colof(tile):
            return ZC0 + tile * 128 if tile < NSL else CC0 + (tile - NSL) * 128
        for dd, nm in ((upD, "upD"), (gluD, "gluD")):
            for c0 in (0, ZC0 + NSL * 128, CC0 + 256):
                w_ = 32 if c0 + 32 <= WUP else WUP - c0
                if w_ <= 0:
                    continue
                dma("sp", dd[:, :, c0:c0 + w_], a_cst.v("p (c n) -> p c n", c=2)[:, :, 0:w_], r=[a_cst], w=[(nm, "z", c0)])

        def load_phaseA_weights(l):
            q = "pool"
            for kc in range(8):
                dma(q, S1W.ap(kc * 1024, (kc + 1) * 1024), w_in[l, kc * 128:(kc + 1) * 128, 0:1024], w=[(S1W, kc * 1024, (kc + 1) * 1024)])
            for kc in range(8):
                dma(q, S3W.ap(kc * 3584, (kc + 1) * 3584), w_in[l, kc * 128:(kc + 1) * 128, 1024:4608], w=[(S3W, kc * 3584, (kc + 1) * 3584)])
            dma(q, PBD.ap(), pbd_d[l], w=[PBD])
            dma(q, WPO.v("p (k n) -> p k n", k=2), wpo_d[l].rearrange("(k p) n -> p k n", p=128), w=[WPO])
            dma(q, WCO.v("p (k n) -> p k n", k=2), wco_d[l].rearrange("(k p) n -> p k n", p=128), w=[WCO])
            dma(q, WAO.ap(), wao_d[l], w=[WAO])
            dma(q, WOUT.v("p (k n) -> p k n", k=8), wout_d[l].rearrange("(k p) n -> p k n", p=128), w=[WOUT])

        def load_phaseB_weights(l):
            q = "pool"
            for kc in range(8):
                dma(q, W1.ap(kc * 4096, (kc + 1) * 4096), w1_d[l, kc * 128:(kc + 1) * 128, :], w=[(W1, kc * 4096, (kc + 1) * 4096)])
            for g in range(4):
                dma(q, W2.v("p (c n) -> p c n", n=1024)[:, g * 8:(g + 1) * 8, :],
                    w2_d[l, g * 1024:(g + 1) * 1024, :].rearrange("(c p) n -> p c n", p=128),
                    w=[(W2, g * 8192, (g + 1) * 8192)])

        def mod_gen(l):
            def D_(oc):
                dma("sp", modst.v("p (k n) -> p k n", k=8), w_mod[l, :, oc * 128:(oc + 1) * 128].rearrange("(k p) n -> p k n", p=128),
                    w=[modst])
            def M_(oc):
                o_ap, o_k = fm_slot()
                mmK(o_ap[:, 0:2], [(modst.ap(kc * 128, (kc + 1) * 128), silc.ap(kc * 2, kc * 2 + 2)) for kc in range(8)],
                    r=[modst, silc], w=[o_k])
                addc = 1.0 if (oc // 8) in (1, 4) else 0.0
                ts("dve", modT[l].ap(oc * 2, oc * 2 + 2), o_ap[:, 0:2], fm[l].ap(FM_BMOD + oc, FM_BMOD + oc + 1), addc,
                   ALU.add, ALU.add, r=[o_k, fm[l]], w=[(modT[l], oc * 2, oc * 2 + 2)])
            D_(0)
            yield
            for oc in range(48):
                M_(oc)
                if oc + 1 < 48:
                    D_(oc + 1)
                yield
        mod_pull = [None]

        def bcast_from_modT(l, vec_idx, j, dstbuf):
            for c in range(8):
                oc = vec_idx * 8 + c
                ts1("dve", diag.ap(), identf.ap(), modT[l].ap(oc * 2 + j, oc * 2 + j + 1), ALU.mult,
                   r=[identf, (modT[l], oc * 2, oc * 2 + 2)], w=[diag])
                o_ap, o_k = fm_slot()
                mmK(o_ap[:, 0:128], [(onesf.ap(), diag.ap())], r=[onesf, diag], w=[o_k])
                act(dstbuf.ap(c * 128, (c + 1) * 128), o_ap[:, 0:128], AF.Copy, r=[o_k], w=[(dstbuf, c * 128, (c + 1) * 128)])

        def row_bcast(l, off, n, dst_ap, wkeys):
            dma("sp", dst_ap, rowv_d[l, :, off:off + n].broadcast_to([128, n]), w=wkeys)

        def stage_S1(l, t0, xsrc, j):
            if not _go():
                return
            c0 = t0 * 128
            dma("sp", a_tab.v("p (a n) -> p a n", a=3), tab[:, 0:3, c0:c0 + T], w=[a_tab])
            if _SUB < 1:
                return
            for ti in range(2):
                t = t0 + ti
                dma("sp", a_xt.ap(), xsrc[t * 128:(t + 1) * 128, :], r=[("x", id(xsrc), t)], w=[a_xt])
                if _SUB2 < 1:
                    continue
                layer_norm_rows(a_xt, a_xn.ap(), [a_xt], [a_xn])
                if _SUB2 < 2:
                    continue
                pt = pT[ti]
                def trs(e, pt=pt):
                    ins = None
                    for c in range(int(os.environ.get('NTR', '8'))):
                        ins = e.transpose(pt[:, c * 128:(c + 1) * 128], a_xn.ap(c * 128, (c + 1) * 128), identb.ap())
                    return ins
                S.add("pe", trs, r=[a_xn, identb], w=[psk("T", ti)])
                if _SUB2 < 3:
                    continue
                for c in range(8):
                    if _SUB2 < 4 and c % 2 == 1:
                        continue
                    sc = modT[l].ap((8 + c) * 2 + j, (8 + c) * 2 + j + 1)
                    sh = modT[l].ap((0 + c) * 2 + j, (0 + c) * 2 + j + 1)
                    dst = a_hT.ap(c * 256 + ti * 128, c * 256 + ti * 128 + 128)
                    wk = [(a_hT, c * 256 + ti * 128, c * 256 + ti * 128 + 128)]
                    if True:
                        act(dst, pt[:, c * 128:(c + 1) * 128], AF.Identity, r=[psk("T", ti), modT[l]], w=wk, bias=sh, scale=sc)
                    else:
                        ts("dve", dst, pt[:, c * 128:(c + 1) * 128], sc, sh, ALU.mult, ALU.add, r=[psk("T", ti), modT[l]], w=wk)
            if _SUB < 2:
                return
            dma("sp", hTD[:, :, c0:c0 + T], a_hT.v("p (c n) -> p c n", c=8), r=[a_hT], w=[("hTD", t0), ("hTD", t0 + 1)])
            hk = lambda kc: a_hT.ap(kc * 256, kc * 256 + 256)
            wS1 = lambda kc, n: S1W.ap(kc * 1024 + n * 128, kc * 1024 + n * 128 + 128)
            wS1k = lambda n: [(S1W, kc * 1024 + n * 128, kc * 1024 + n * 128 + 128) for kc in range(8)]
            bfm = lambda n: fm[l].ap(FM_BIN + n, FM_BIN + n + 1)
            cosA = a_tab.ap(0, 256); sinA = a_tab.ap(256, 512); vmA = a_tab.ap(512, 768)
            if _SUB < 3:
                return
            o_ap, o_k = fm_slot()
            mmK(o_ap, [(wS1(kc, 0), hk(kc)) for kc in range(8)], r=[a_hT] + wS1k(0), w=[o_k])
            act(a_t1.ap(), o_ap, AF.Identity, r=[o_k, fm[l]], w=[a_t1], bias=bfm(0))
            o2, o2k = fm_slot()
            mmK(o2, [(permf.ap(), a_t1.ap())], r=[permf, a_t1], w=[o2k])
            tt("dve", a_t2.ap(), o2, sinA, ALU.mult, r=[o2k, a_tab], w=[a_t2])
            tt("dve", a_t1.ap(), a_t1.ap(), cosA, ALU.mult, r=[a_t1, a_tab], w=[a_t1])
            for ti in range(2):
                kc_ = kvcol(t0 + ti)
                tt("dve", kT.ap(kc_, kc_ + 128), a_t1.ap(ti * 128, ti * 128 + 128), a_t2.ap(ti * 128, ti * 128 + 128), ALU.add,
                   r=[a_t1, a_t2], w=[(kT, kc_, kc_ + 128)])
            if _SUB < 4:
                return
            for ti in range(2):
                t = t0 + ti
                o_ap, o_k = fm_slot()
                mmK(o_ap[:, 0:128], [(a_hT.ap(kc * 256 + ti * 128, kc * 256 + ti * 128 + 128), wS1(kc, 1)) for kc in range(8)],
                    r=[a_hT] + wS1k(1), w=[o_k])
                tt("dve", Vb.ap(kvcol(t), kvcol(t) + 128), o_ap[:, 0:128], bvb.ap(), ALU.add, r=[o_k, bvb], w=[(Vb, kvcol(t), kvcol(t) + 128)])
            if _SUB < 5:
                return
            for c in range(2):
                o_ap, o_k = fm_slot()
                mmK(o_ap, [(wS1(kc, 2 + c), hk(kc)) for kc in range(8)], r=[a_hT] + wS1k(2 + c), w=[o_k])
                stt("dve", a_upst.ap(c * 256, (c + 1) * 256), o_ap, bfm(2 + c), vmA, ALU.add, ALU.mult,
                    r=[o_k, fm[l], a_tab], w=[(a_upst, c * 256, (c + 1) * 256)])
            col = colof(t0)
            dma("sp", upD[:, :, col:col + T], a_upst.v("p (c n) -> p c n", c=2), r=[a_upst], w=[("upD", t0), ("upD", t0 + 1)])
            if _SUB < 6:
                return
            for c in range(2):
                ob, obk = fm_slot()
                mmK(ob, [(wS1(kc, 6 + c), hk(kc)) for kc in range(8)], r=[a_hT] + wS1k(6 + c), w=[obk])
                act(a_sgb.ap(), ob, AF.Sigmoid, r=[obk, fm[l]], w=[a_sgb], bias=bfm(6 + c))
                tt("dve", a_sgb.ap(), a_sgb.ap(), vmA, ALU.mult, r=[a_sgb, a_tab], w=[a_sgb])
                oa, oak = fm_slot()
                mmK(oa, [(wS1(kc, 4 + c), hk(kc)) for kc in range(8)], r=[a_hT] + wS1k(4 + c), w=[oak])
                stt("dve", a_glust.ap(c * 256, (c + 1) * 256), oa, bfm(4 + c), a_sgb.ap(), ALU.add, ALU.mult,
                    r=[oak, fm[l], a_sgb], w=[(a_glust, c * 256, (c + 1) * 256)])
            dma("sp", gluD[:, :, col:col + T], a_glust.v("p (c n) -> p c n", c=2), r=[a_glust], w=[("gluD", t0), ("gluD", t0 + 1)])

        def stage_FULL(l, t0, xsrc, xdst, is_ctx):
            if not _go():
                return
            c0 = t0 * 128
            col = colof(t0)
            nb = [t0 - 1, t0, t0 + 1, t0 + 2]
            nbk = lambda nm: [(nm, t) for t in nb] + [(nm, "z", 0), (nm, "z", ZC0 + NSL * 128), (nm, "z", CC0 + 256)]
            dma("sp", f_hT2.v("p (c n) -> p c n", c=8), hTD[:, :, c0:c0 + T], r=[("hTD", t0), ("hTD", t0 + 1)], w=[f_hT2])
            dma("sp", f_tab.v("p (a n) -> p a n", a=4)[:, 0:2, :], tab[:, 0:2, c0:c0 + T], w=[(f_tab, 0, 512)])
            dma("sp", f_tab.v("p (a n) -> p a n", a=4)[:, 2:4, :], tab[:, 3:5, c0:c0 + T], w=[(f_tab, 512, 1024)])
            dma("sp", f_upW.v("p (c n) -> p c n", c=2), upD[:, :, col - 8:col + 264], r=nbk("upD"), w=[f_upW])
            dma("sp", f_gluW.v("p (c n) -> p c n", c=2)[:, :, 0:286], gluD[:, :, col - 15:col + 271], r=nbk("gluD"), w=[f_gluW])
            h2 = lambda kc: f_hT2.ap(kc * 256, kc * 256 + 256)
            wS3 = lambda kc, n: S3W.ap(kc * 3584 + n * 128, kc * 3584 + n * 128 + 128)
            wS3k = lambda n: [(S3W, kc * 3584 + n * 128, kc * 3584 + n * 128 + 128) for kc in range(8)]
            bfm = lambda n: fm[l].ap(FM_BIN + n, FM_BIN + n + 1)
            cosF = f_tab.ap(0, 256); sinF = f_tab.ap(256, 512)
            for jq in range(4):
                o_ap, o_k = fm_slot()
                mmK(o_ap, [(wS3(kc, jq), h2(kc)) for kc in range(8)], r=[f_hT2] + wS3k(jq), w=[o_k])
                qf = q_f.ap(jq * 256, (jq + 1) * 256)
                qfk = [(q_f, jq * 256, (jq + 1) * 256)]
                act(qf, o_ap, AF.Identity, r=[o_k, fm[l]], w=qfk, bias=bfm(8 + jq))
                o2, o2k = fm_slot()
                mmK(o2, [(permf.ap(), qf)], r=[permf] + qfk, w=[o2k])
                tt("dve", q_t2.ap(), o2, sinF, ALU.mult, r=[o2k, f_tab], w=[q_t2])
                tt("dve", q_t1.ap(), qf, cosF, ALU.mult, r=qfk + [f_tab], w=[q_t1])
                tt("dve", f_qT.ap(jq * 256, (jq + 1) * 256), q_t1.ap(), q_t2.ap(), ALU.add, r=[q_t1, q_t2],
                   w=[(f_qT, jq * 256, (jq + 1) * 256)])
            pti = [0]
            for ti in range(2):
                t = t0 + ti
                if is_ctx:
                    klist = [(NSL, None), (NSL + 1, None)]
                else:
                    klist = [(t - 1, mprev), (t, None), (t + 1, mnext), (NSL, None), (NSL + 1, None)]
                for h in range(2):
                    po, pok = pF[4], psk("F", 4)
                    pd, pdk = pF[5], psk("F", 5)
                    hp0, hp1 = h * 64, h * 64 + 64
                    qv = f_qT.v("p (g n) -> p g n", g=4, p0=hp0, p1=hp1)[:, :, ti * 128:(ti + 1) * 128]
                    for ki, (kt, msk) in enumerate(klist):
                        s_ap, s_k = big_slot()
                        mmK(s_ap[:, :].rearrange("p (g n) -> p g n", g=4),
                            [(kT.ap(kvcol(kt), kvcol(kt) + 128, hp0, hp1), qv)],
                            r=[(kT, kvcol(kt), kvcol(kt) + 128), f_qT], w=[s_k])
                        P = f_PT[pti[0] % 3]
                        pti[0] += 1
                        act(P.ap(), s_ap[:, :], AF.Exp, r=[s_k, kbias], w=[P], bias=kbias.ap(kt, kt + 1), scale=0.125)
                        if msk is not None:
                            tt("dve", P.v("p (g n) -> p g n", g=4), P.v("p (g n) -> p g n", g=4),
                               msk.ap().unsqueeze(1).broadcast_to([128, 4, 128]), ALU.mult, r=[P, msk], w=[P])
                        first = ki == 0
                        last = ki == len(klist) - 1
                        S.add("pe", lambda e, kt=kt, P=P, first=first, last=last, po=po: e.matmul(
                            po[:, :], lhsT=Vb.ap(kvcol(kt), kvcol(kt) + 128), rhs=P.ap(), start=first, stop=last),
                            r=[(Vb, kvcol(kt), kvcol(kt) + 128), P], w=[pok])
                        S.add("pe", lambda e, P=P, first=first, last=last, pd=pd: e.matmul(
                            pd[:, :], lhsT=onesb.ap(), rhs=P.ap(), start=first, stop=last),
                            r=[onesb, P], w=[pdk])
                    den = at_den.v("p (g n) -> p g n", g=4, p0=hp0, p1=hp1)
                    tt("dve", den, pd[hp0:hp1, :].rearrange("p (g n) -> p g n", g=4),
                       esink.ap(0, 4, hp0, hp1).unsqueeze(2).broadcast_to([64, 4, 128]), ALU.add, r=[pdk, esink], w=[at_den])
                    S.add("dve", lambda e, den=den: e.reciprocal(out=den, in_=den), r=[at_den], w=[at_den])
                    tt("dve", f_oT.v("p (g n) -> p g n", g=4, p0=hp0, p1=hp1)[:, :, ti * 128:(ti + 1) * 128],
                       po[hp0:hp1, :].rearrange("p (g n) -> p g n", g=4), den, ALU.mult, r=[pok, at_den], w=[f_oT])
            u3 = f_upW.v("p (c n) -> p c n", c=2)
            pa3 = p_a.v("p (c n) -> p c n", c=2)
            pb3 = p_b.v("p (c n) -> p c n", c=2)
            pinv = lambda c, hp0, hp1: f_tab.ap(512 + c * 256, 512 + (c + 1) * 256, hp0, hp1)
            def pool_d(src3, c, half):
                hp0, hp1 = half * 64, half * 64 + 64
                tt("dve", p_tmp.ap(0, 256, hp0, hp1), src3[hp0:hp1, c, 8:264], pinv(c, hp0, hp1), ALU.mult,
                   r=[p_a, p_b, f_tab], w=[p_tmp])
                tt("dve", f_dT.ap(c * 256, (c + 1) * 256, hp0, hp1), p_tmp.ap(0, 256, hp0, hp1), u3[hp0:hp1, c, 8:264],
                   ALU.subtract, r=[p_tmp, f_upW], w=[f_dT])
            tt("dve", pa3[:, :, 1:272], u3[:, :, 0:271], u3[:, :, 1:272], ALU.add, r=[f_upW], w=[p_a])
            pool_d(pa3, 0, 0)
            tt("dve", pb3[:, :, 2:271], pa3[:, :, 1:270], pa3[:, :, 3:272], ALU.add, r=[p_a], w=[p_b])
            pool_d(pb3, 0, 1)
            tt("dve", pa3[:, :, 4:269], pb3[:, :, 2:267], pb3[:, :, 6:271], ALU.add, r=[p_b], w=[p_a])
            pool_d(pa3, 1, 0)
            tt("dve", pb3[:, :, 8:265], pa3[:, :, 4:261], pa3[:, :, 12:269], ALU.add, r=[p_a], w=[p_b])
            pool_d(pb3, 1, 1)
            for c in range(2):
                o_ap, o_k = fm_slot()
                mmK(o_ap, [(PBD.ap(c * 128, (c + 1) * 128), f_dT.ap(c * 256, (c + 1) * 256))], r=[PBD, f_dT], w=[o_k])
                act(f_ypT.ap(c * 256, (c + 1) * 256), o_ap, AF.Identity, r=[o_k, fm[l]], w=[(f_ypT, c * 256, (c + 1) * 256)],
                    scale=fm[l].ap(FM_PS + c, FM_PS + c + 1))
            g3 = f_gluW.v("p (c n) -> p c n", c=2)
            for c in range(2):
                cw = lambda jj: fm[l].ap(FM_CW + c * 31 + jj, FM_CW + c * 31 + jj + 1)
                a0 = c_acc0.ap(c * 256, (c + 1) * 256); a1 = c_acc1.ap(c * 256, (c + 1) * 256)
                k0 = [(c_acc0, c * 256, (c + 1) * 256)]; k1 = [(c_acc1, c * 256, (c + 1) * 256)]
                ts("dve", a0, g3[:, c, 0:256], cw(0), fm[l].ap(FM_CB + c, FM_CB + c + 1), ALU.mult, ALU.add, r=[f_gluW, fm[l]], w=k0)
                ts1("dve", a1, g3[:, c, 1:257], cw(1), ALU.mult, r=[f_gluW, fm[l]], w=k1)
                for jj in range(2, 31):
                    aa, kk = (a0, k0) if jj % 2 == 0 else (a1, k1)
                    stt("dve", aa, g3[:, c, jj:jj + 256], cw(jj), aa, ALU.mult, ALU.add, r=[f_gluW, fm[l]] + kk, w=kk)
                tt("dve", a0, a0, a1, ALU.add, r=k0 + k1, w=k0)
                act(a1, a0, AF.Square, r=k0, w=k1)
            pm_, pmk = fm_slot()
            mmK(pm_, [(onesLN.ap(), c_acc0.ap(c * 256, (c + 1) * 256)) for c in range(2)], r=[onesLN, c_acc0], w=[pmk])
            pe_, pek = fm_slot()
            mmK(pe_, [(onesLN.ap(), c_acc1.ap(c * 256, (c + 1) * 256)) for c in range(2)], r=[onesLN, c_acc1], w=[pek])
            act(c_mean.ap(), pm_, AF.Copy, r=[pmk], w=[c_mean])
            tt("dve", c_var.ap(), c_mean.ap(), c_mean.ap(), ALU.mult, r=[c_mean], w=[c_var])
            tt("dve", c_var.ap(), pe_, c_var.ap(), ALU.subtract, r=[pek, c_var], w=[c_var])
            ts1("dve", c_var.ap(), c_var.ap(), 0.0, ALU.max, r=[c_var], w=[c_var])
            act(c_var.ap(), c_var.ap(), AF.Sqrt, r=[c_var], w=[c_var], bias=eps_ap())
            S.add("dve", lambda e: e.reciprocal(out=c_var.ap(), in_=c_var.ap()), r=[c_var], w=[c_var])
            for c in range(2):
                zc = c_z.ap(c * 256, (c + 1) * 256); zk = [(c_z, c * 256, (c + 1) * 256)]
                tt("dve", zc, c_acc0.ap(c * 256, (c + 1) * 256), c_mean.ap(), ALU.subtract, r=[c_acc0, c_mean], w=zk)
                tt("dve", zc, zc, c_var.ap(), ALU.mult, r=zk + [c_var], w=zk)
                act(f_sT.ap(c * 256, (c + 1) * 256), zc, AF.Silu, r=zk + [fm[l]], w=[(f_sT, c * 256, (c + 1) * 256)],
                    bias=fm[l].ap(FM_CBETA + c, FM_CBETA + c + 1), scale=fm[l].ap(FM_CG + c, FM_CG + c + 1))
            for c in range(8):
                if mod_pull[0] is not None:
                    next(mod_pull[0], None)
                for b in range(3):
                    n = 4 + c * 3 + b
                    o_ap, o_k = fm_slot()
                    mmK(o_ap, [(wS3(kc, n), h2(kc)) for kc in range(8)], r=[f_hT2] + wS3k(n), w=[o_k])
                    act(m_gs.ap(b * 256, (b + 1) * 256), o_ap, AF.Sigmoid, r=[o_k, fm[l]], w=[(m_gs, b * 256, (b + 1) * 256)],
                        bias=bfm(8 + n))
                oA, oAk = fm_slot()
                mmK(oA, [(WPO.ap(kc * 1024 + c * 128, kc * 1024 + c * 128 + 128), f_ypT.ap(kc * 256, (kc + 1) * 256)) for kc in range(2)],
                    r=[WPO, f_ypT], w=[oAk])
                tt("dve", m_1.ap(), oA, m_gs.ap(0, 256), ALU.mult, r=[oAk, (m_gs, 0, 256)], w=[m_1])
                oB, oBk = fm_slot()
                mmK(oB, [(WAO.ap(g * 1024 + c * 128, g * 1024 + c * 128 + 128), f_oT.ap(g * 256, (g + 1) * 256)) for g in range(4)],
                    r=[WAO, f_oT], w=[oBk])
                tt("dve", m_2.ap(), oB, m_gs.ap(256, 512), ALU.mult, r=[oBk, (m_gs, 256, 512)], w=[m_2])
                tt("dve", m_1.ap(), m_1.ap(), m_2.ap(), ALU.add, r=[m_1, m_2], w=[m_1])
                oC, oCk = fm_slot()
                mmK(oC, [(WCO.ap(kc * 1024 + c * 128, kc * 1024 + c * 128 + 128), f_sT.ap(kc * 256, (kc + 1) * 256)) for kc in range(2)],
                    r=[WCO, f_sT], w=[oCk])
                tt("dve", m_2.ap(), oC, m_gs.ap(512, 768), ALU.mult, r=[oCk, (m_gs, 512, 768)], w=[m_2])
                tt("dve", f_mT.ap(c * 256, (c + 1) * 256), m_1.ap(), m_2.ap(), ALU.add, r=[m_1, m_2], w=[(f_mT, c * 256, (c + 1) * 256)])
            for ti in range(2):
                t = t0 + ti
                dma("sp", f_xt2.ap(), xsrc[t * 128:(t + 1) * 128, :], r=[("x", id(xsrc), t)], w=[f_xt2])
                for hf in range(2):
                    y_ap, y_k = big_slot()
                    mmK(y_ap[:, :], [(f_mT.ap(kc * 256 + ti * 128, kc * 256 + ti * 128 + 128),
                                      WOUT.ap(kc * 1024 + hf * 512, kc * 1024 + hf * 512 + 512)) for kc in range(8)],
                        r=[f_mT, WOUT], w=[y_k])
                    tt("dve", f_tt.ap(hf * 512, (hf + 1) * 512), y_ap[:, :], BC[0].ap(hf * 512, (hf + 1) * 512), ALU.mult,
                       r=[y_k, BC[0]], w=[(f_tt, hf * 512, (hf + 1) * 512)])
                stt("dve", f_tt.ap(), f_xt2.ap(), ALPHA, f_tt.ap(), ALU.mult, ALU.add, r=[f_xt2, f_tt], w=[f_tt])
                layer_norm_rows(f_tt, f_tt.ap(), [f_tt], [f_tt])
                tt("dve", f_tt.ap(), f_tt.ap(), BC[1].ap(), ALU.mult, r=[f_tt, BC[1]], w=[f_tt])
                tt("dve", f_tt.ap(), f_tt.ap(), BC[2].ap(), ALU.add, r=[f_tt, BC[2]], w=[f_tt])
                dma("sp", xdst[t * 128:(t + 1) * 128, :], f_tt.ap(), r=[f_tt], w=[("x", id(xdst), t)])
                if debug and l == 0:
                    dma("sp", dbg["x1"][t * 128:(t + 1) * 128, :], f_tt.ap(), r=[f_tt], is_out=True)

        def stage_B(l, t0, xsrc, xdst, j, out_rows=None):
            if not _go():
                return
            for ti in range(2):
                t = t0 + ti
                dma("sp", b_xt.ap(), xsrc[t * 128:(t + 1) * 128, :], r=[("x", id(xsrc), t)], w=[b_xt])
                layer_norm_rows(b_xt, b_xn.ap(), [b_xt], [b_xn])
                pt = pT[ti]
                def trs(e, pt=pt):
                    ins = None
                    for c in range(8):
                        ins = e.transpose(pt[:, c * 128:(c + 1) * 128], b_xn.ap(c * 128, (c + 1) * 128), identb.ap())
                    return ins
                S.add("pe", trs, r=[b_xn, identb], w=[psk("T", ti)])
                for c in range(8):
                    sc = modT[l].ap((32 + c) * 2 + j, (32 + c) * 2 + j + 1)
                    sh = modT[l].ap((24 + c) * 2 + j, (24 + c) * 2 + j + 1)
                    dst = b_hT.ap(c * 256 + ti * 128, c * 256 + ti * 128 + 128)
                    wk = [(b_hT, c * 256 + ti * 128, c * 256 + ti * 128 + 128)]
                    if True:
                        act(dst, pt[:, c * 128:(c + 1) * 128], AF.Identity, r=[psk("T", ti), modT[l]], w=wk, bias=sh, scale=sc)
                    else:
                        ts("dve", dst, pt[:, c * 128:(c + 1) * 128], sc, sh, ALU.mult, ALU.add, r=[psk("T", ti), modT[l]], w=wk)
                if ti == 0:
                    stt("dve", b_pre.ap(), b_xt.ap(), ALPHA, BC[1].ap(), ALU.mult, ALU.add, r=[b_xt, BC[1]], w=[b_pre])
            for c in range(32):
                o_ap, o_k = fm_slot()
                mmK(o_ap, [(W1.ap(kc * 4096 + c * 128, kc * 4096 + c * 128 + 128), b_hT.ap(kc * 256, (kc + 1) * 256)) for kc in range(8)],
                    r=[b_hT] + [(W1, kc * 4096 + c * 128, kc * 4096 + c * 128 + 128) for kc in range(8)], w=[o_k])
                rb = b_r[c % 3]
                act(rb.ap(), o_ap, AF.Relu, r=[o_k, fm[l]], w=[rb], bias=fm[l].ap(FM_B1 + c, FM_B1 + c + 1))
                tt("dve", b_fT.ap(c * 256, (c + 1) * 256), rb.ap(), rb.ap(), ALU.mult, r=[rb], w=[(b_fT, c * 256, (c + 1) * 256)])
            for ti in range(2):
                t = t0 + ti
                if ti == 1:
                    stt("dve", b_pre.ap(), b_xt.ap(), ALPHA, BC[1].ap(), ALU.mult, ALU.add, r=[b_xt, BC[1]], w=[b_pre])
                for hf in range(2):
                    y_ap, y_k = pF[2 + hf], psk("F", 2 + hf)
                    mmK(y_ap[:, :], [(b_fT.ap(c * 256 + ti * 128, c * 256 + ti * 128 + 128),
                                      W2.ap(c * 1024 + hf * 512, c * 1024 + hf * 512 + 512)) for c in range(32)],
                        r=[b_fT, W2], w=[y_k])
                    tt("dve", b_tt.ap(hf * 512, (hf + 1) * 512), y_ap[:, :], BC[0].ap(hf * 512, (hf + 1) * 512), ALU.mult,
                       r=[y_k, BC[0]], w=[(b_tt, hf * 512, (hf + 1) * 512)])
                tt("dve", b_tt.ap(), b_tt.ap(), b_pre.ap(), ALU.add, r=[b_tt, b_pre], w=[b_tt])
                layer_norm_rows(b_tt, b_tt.ap(), [b_tt], [b_tt])
                tt("dve", b_tt.ap(), b_tt.ap(), BC[2].ap(), ALU.mult, r=[b_tt, BC[2]], w=[b_tt])
                tt("dve", b_tt.ap(), b_tt.ap(), BC[3].ap(), ALU.add, r=[b_tt, BC[3]], w=[b_tt])
                if out_rows is not None:
                    r0 = out_rows + ti * 128
                    dma("sp", out_d[r0:r0 + 128, :], b_tt.ap(), r=[b_tt], is_out=True)
                else:
                    dma("sp", xdst[t * 128:(t + 1) * 128, :], b_tt.ap(), r=[b_tt], w=[("x", id(xdst), t)])
                    if debug:
                        dma("sp", dbg["x2"][t * 128:(t + 1) * 128, :], b_tt.ap(), r=[b_tt], is_out=True)

        load_phaseA_weights(0)
        for _ in mod_gen(0):
            pass
        xcur = xin
        for l in range(DEPTH):
            last = l == DEPTH - 1
            row_bcast(l, 0, 128, bvb.ap(), [bvb])
            act(esink.ap(0, 4), fm[l].ap(FM_SINK, FM_SINK + 4), AF.Exp, r=[fm[l]], w=[esink])
            row_bcast(l, 128, 1024, BC[1].ap(), [BC[1]])
            row_bcast(l, 128 + 1024, 1024, BC[2].ap(), [BC[2]])
            mix_lo, mix_hi = (0, 20) if l == 0 else (1, 19)
            if not last:
                mod_pull[0] = mod_gen(l + 1)
            stage_S1(l, NSL, xcur, 1)
            if not last:
                bcast_from_modT(l, 2, 1, BC[0])
                stage_FULL(l, NSL, xcur, x1D, True)
            bcast_from_modT(l, 2, 0, BC[0])
            s1_pairs = list(range(mix_lo, mix_hi, 2))
            full_pairs = list(range(mix_lo + 1, mix_hi - 1, 2))
            stage_S1(l, s1_pairs[0], xcur, 0)
            for i, fp in enumerate(full_pairs):
                stage_S1(l, s1_pairs[i + 1], xcur, 0)
                stage_FULL(l, fp, xcur, x1D, False)
            if mod_pull[0] is not None:
                for _ in mod_pull[0]:
                    pass
                mod_pull[0] = None
            load_phaseB_weights(l)
            row_bcast(l, 128 + 3 * 1024, 1024, BC[2].ap(), [BC[2]])
            row_bcast(l, 128 + 4 * 1024, 1024, BC[3].ap(), [BC[3]])
            def prep_B(j):
                bcast_from_modT(l, 5, j, BC[0])
                row_bcast(l, 128 + 2 * 1024, 1024, BC[1].ap(), [BC[1]])
                tt("dve", BC[1].ap(), BC[1].ap(), BC[0].ap(), ALU.mult, r=[BC[1], BC[0]], w=[BC[1]])
            if not last:
                prep_B(1)
                stage_B(l, NSL, x1D, x2D, 1)
            prep_B(0)
            for fp in full_pairs:
                stage_B(l, fp, x1D, x2D, 0, out_rows=((fp - 2) * 128 if last else None))
            if not last:
                load_phaseA_weights(l + 1)
            xcur = x2D
        S.emit(nc, st)
    build_program.last_sched = S
    return nc

SPLIT_POOL = 256
SPLIT_Q = 768
SPLIT_K = 896
SPLIT_V = 1024
SPLIT_CONV = 1536


def _host_prep(inp):
    f32 = np.float32
    w_in = np.asarray(inp["w_in"], f32)
    b_in = np.asarray(inp["b_in"], f32)
    cols = []
    cols += list(range(SPLIT_Q, SPLIT_K))
    cols += list(range(SPLIT_K, SPLIT_V))
    cols += list(range(0, 256))
    cols += list(range(SPLIT_V, SPLIT_V + 512))
    for j in range(4):
        for h in range(2):
            hd = h * 4 + j
            cols += list(range(SPLIT_POOL + hd * 64, SPLIT_POOL + hd * 64 + 64))
    for c in range(8):
        for b in range(3):
            cols += list(range(SPLIT_CONV + b * 1024 + c * 128, SPLIT_CONV + b * 1024 + c * 128 + 128))
    cols = np.asarray(cols)
    assert cols.shape[0] == 36 * 128
    w_in_r = np.ascontiguousarray(w_in[:, :, cols])
    b_in_r = b_in[:, cols]
    fm = np.zeros((DEPTH, 128, NFM), f32)
    rowv = np.zeros((DEPTH, 1, NROW), f32)
    pbd = np.zeros((DEPTH, 128, 256), f32)
    wao_r = np.zeros((DEPTH, 128, 4 * D), f32)
    for l in range(DEPTH):
        fm[l, :, FM_BIN:FM_BIN + 36] = b_in_r[l].reshape(36, 128).T
        fm[l, :, FM_B1:FM_B1 + 32] = np.asarray(inp["b_mlp1"], f32)[l].reshape(32, 128).T
        fm[l, :, FM_BMOD:FM_BMOD + 48] = np.asarray(inp["b_mod"], f32)[l].reshape(48, 128).T
        fm[l, :, FM_PS:FM_PS + 2] = np.asarray(inp["pool_scale"], f32)[l].reshape(2, 128).T
        fm[l, :, FM_CB:FM_CB + 2] = np.asarray(inp["conv_b"], f32)[l].reshape(2, 128).T
        fm[l, :, FM_CG:FM_CG + 2] = np.asarray(inp["conv_ln_g"], f32)[l].reshape(2, 128).T
        fm[l, :, FM_CBETA:FM_CBETA + 2] = np.asarray(inp["conv_ln_b"], f32)[l].reshape(2, 128).T
        cw = np.asarray(inp["conv_w"], f32)[l]
        for c in range(2):
            fm[l, :, FM_CW + c * 31:FM_CW + (c + 1) * 31] = cw[:, c * 128:(c + 1) * 128].T
        sink = np.asarray(inp["attn_sink"], f32)[l]
        for h in range(2):
            fm[l, h * 64:(h + 1) * 64, FM_SINK:FM_SINK + 4] = sink[h * 4:(h + 1) * 4][None, :]
        rowv[l, 0, 0:128] = b_in[l, SPLIT_K:SPLIT_V]
        rowv[l, 0, 128:1152] = np.asarray(inp["ln1_g"], f32)[l]
        rowv[l, 0, 1152:2176] = np.asarray(inp["ln1_b"], f32)[l]
        rowv[l, 0, 2176:3200] = np.asarray(inp["b_mlp2"], f32)[l]
        rowv[l, 0, 3200:4224] = np.asarray(inp["ln2_g"], f32)[l]
        rowv[l, 0, 4224:5248] = np.asarray(inp["ln2_b"], f32)[l]
        pw = np.asarray(inp["pool_w"], f32)[l]
        for c in range(2):
            for gl in range(2):
                pbd[l, gl * 64:(gl + 1) * 64, c * 128 + gl * 64:c * 128 + (gl + 1) * 64] = pw[2 * c + gl]
        wao = np.asarray(inp["w_attn_out"], f32)[l]
        for h in range(2):
            for g in range(4):
                wao_r[l, h * 64:(h + 1) * 64, g * D:(g + 1) * D] = wao[(h * 4 + g) * 64:(h * 4 + g + 1) * 64, :]
    consts = np.zeros((128, 4, 128), f32)
    consts[:, 0, :] = np.eye(128, dtype=f32)
    p = np.arange(128)
    partner = np.where((p % 32) < 16, p + 16, p - 16)
    consts[partner, 1, p] = 1.0
    kk = np.arange(128)[:, None]
    qq = np.arange(128)[None, :]
    consts[:, 2, :] = (kk >= qq).astype(f32)
    consts[:, 3, :] = (kk <= qq).astype(f32)
    shared = dict(
        consts=consts, fm=fm, rowv=rowv, w_mod=np.ascontiguousarray(np.asarray(inp["w_mod"], f32)),
        w_in_r=w_in_r, pbd=pbd, w_pool_out=np.ascontiguousarray(np.asarray(inp["w_pool_out"], f32)),
        wao_r=wao_r, w_conv_out=np.ascontiguousarray(np.asarray(inp["w_conv_out"], f32)),
        w_out=np.ascontiguousarray(np.asarray(inp["w_out"], f32)),
        w_mlp1=np.ascontiguousarray(np.asarray(inp["w_mlp1"], f32)),
        w_mlp2=np.ascontiguousarray(np.asarray(inp["w_mlp2"], f32)),
    )
    return shared


def _core_tables(a):
    f32 = np.float32
    pos = a - 256 + np.arange(NSL * 128)
    valid = (pos >= 0) & (pos < L)
    tab = np.zeros((128, 5, NT * 128), f32)
    p = np.arange(128)
    d = p % 64
    inv = (10000.0 ** (-(np.arange(16, dtype=f32)) / 16.0)).astype(f32)
    pc = np.clip(pos, 0, L - 1)
    rowp = (pc // 64).astype(f32)
    colp = (pc % 64).astype(f32)
    posm = np.where((d < 32)[:, None], rowp[None, :], colp[None, :]).astype(f32)
    ang = (posm * inv[d % 16][:, None]).astype(f32)
    sgn = np.where((d % 32) < 16, -1.0, 1.0).astype(f32)
    tab[:, 0, :NSL * 128] = np.cos(ang)
    tab[:, 1, :NSL * 128] = np.sin(ang) * sgn[:, None]
    tab[:, 0, NSL * 128:] = 1.0
    tab[:, 2, :NSL * 128] = valid.astype(f32)[None, :]
    tab[:, 2, NSL * 128:] = 1.0

    def pinv(t, w, Ls):
        lo = np.clip(t - w // 2, 0, Ls)
        hi = np.clip(t + w - w // 2, 0, Ls)
        cnt = (hi - lo).astype(f32)
        return np.where(cnt > 0, 1.0 / np.maximum(cnt, 1.0), 0.0).astype(f32)
    tc = np.arange(CTX)
    for ci, (wlo, whi) in enumerate(((2, 4), (8, 16))):
        tab[0:64, 3 + ci, :NSL * 128] = pinv(pos, wlo, L)[None, :]
        tab[64:128, 3 + ci, :NSL * 128] = pinv(pos, whi, L)[None, :]
        tab[0:64, 3 + ci, NSL * 128:] = pinv(tc, wlo, CTX)[None, :]
        tab[64:128, 3 + ci, NSL * 128:] = pinv(tc, whi, CTX)[None, :]
    kb = np.zeros((128, 32), f32)
    v2 = valid.reshape(NSL, 128).T
    kb[:, :NSL] = np.where(v2, 0.0, -30000.0)
    return tab, kb


_NC_CACHE = {}


def kernel(**inp):
    f32 = np.float32
    x = np.asarray(inp["x"], f32)
    c = np.asarray(inp["c"], f32)
    ctx = np.asarray(inp["ctx"], f32)
    c_ctx = np.asarray(inp["c_ctx"], f32)
    B = x.shape[0]
    shared = _host_prep(inp)
    xpad = np.zeros((B, L + 512, D), f32)
    xpad[:, 256:256 + L] = x
    in_maps = []
    for r in range(8):
        b, q = r // 4, r % 4
        a = q * 2048
        xin = np.concatenate([xpad[b, a:a + NSL * 128], ctx[b]], axis=0)
        tab, kb = _core_tables(a)
        cvec = np.zeros((128, 16), f32)
        cvec[:, 0::2] = c[b].reshape(8, 128).T
        cvec[:, 1::2] = c_ctx.reshape(8, 128).T
        m = dict(shared)
        m.update(xin=np.ascontiguousarray(xin), tab=tab, kbias=kb, cvec=cvec)
        in_maps.append(m)
    if "nc" not in _NC_CACHE:
        _NC_CACHE["nc"] = build_program()
    nc = _NC_CACHE["nc"]
    res = run_bass_kernel_spmd(nc, in_maps, core_ids=list(range(8)))
    out = np.zeros((B, L, D), f32)
    for r in range(8):
        b, q = r // 4, r % 4
        out[b, q * 2048:(q + 1) * 2048] = np.asarray(res.results[r]["out"], f32)
    return out
```

```python
import math, os
_SUB = int(os.environ.get('SUB', '99'))
_SUB2 = int(os.environ.get('SUB2', '99'))
import numpy as np
from contextlib import ExitStack
import concourse.bass as bass
import concourse.mybir as mybir
from concourse.bass_utils import run_bass_kernel_spmd
F32 = mybir.dt.float32
BF16 = mybir.dt.bfloat16
ALU = mybir.AluOpType
AF = mybir.ActivationFunctionType
AX = mybir.AxisListType

ENGS = ("pe", "act", "dve", "pool", "sp")
GRAN = 128
SEM_CAP = 30000
N_DMA_SEM = 40
N_SW_SEM = 8
SAME_ENG_WINDOW = 6


class Buf:
    def __init__(self, tensor, tname, off, n, parts=128):
        self.t = tensor
        self.tname = tname
        self.off = off
        self.n = n
        self.parts = parts

    def ap(self, lo=0, hi=None, p0=0, p1=None):
        hi = self.n if hi is None else hi
        p1 = self.parts if p1 is None else p1
        return self.t[p0:p1, self.off + lo:self.off + hi]

    def v(self, pattern=None, lo=0, hi=None, p0=0, p1=None, **kw):
        a = self.ap(lo, hi, p0, p1)
        if pattern:
            a = a.rearrange(pattern, **kw)
        return a

    def keys(self, lo=0, hi=None):
        hi = self.n if hi is None else hi
        g0 = (self.off + lo) // GRAN
        g1 = (self.off + hi - 1) // GRAN
        return [(self.tname, g) for g in range(g0, g1 + 1)]


def _keys(items):
    out = []
    for it in items:
        if isinstance(it, Buf):
            out.extend(it.keys())
        elif isinstance(it, tuple) and len(it) == 3 and isinstance(it[0], Buf):
            out.extend(it[0].keys(it[1], it[2]))
        else:
            out.append(it)
    return out


class Op:
    __slots__ = ("eng", "fn", "deps", "idx", "is_dma", "dsem", "dval", "sig",
                 "rank", "waits", "gidx")


class Sched:
    def __init__(self):
        self.ops = {e: [] for e in ENGS}
        self.state = {}
        self.ndma = 0
        self.nsw = 0
        self.dma_last = [None] * N_DMA_SEM
        self.dma_cnt = [0] * N_DMA_SEM
        self.gcount = 0
        self.out_dmas = []

    def add(self, eng, fn, r=(), w=(), dma=False, out=False):
        o = Op()
        o.eng = eng
        o.fn = fn
        o.is_dma = dma
        o.idx = len(self.ops[eng])
        o.deps = set()
        o.sig = False
        o.gidx = self.gcount
        self.gcount += 1
        rk = _keys(r)
        wk = _keys(w)
        st = self.state
        for k in rk:
            s = st.get(k)
            if s is not None and s[0] is not None:
                o.deps.add(s[0])
        for k in wk:
            s = st.get(k)
            if s is not None:
                if s[0] is not None:
                    o.deps.add(s[0])
                o.deps.update(s[1].values())
        for k in rk:
            s = st.get(k)
            if s is None:
                s = st[k] = [None, {}]
            s[1][(eng, o.idx) if dma else eng] = o
        for k in wk:
            st[k] = [o, {}]
        if dma:
            if eng == "pool":
                si = self.nsw % N_SW_SEM
                self.nsw += 1
            else:
                si = N_SW_SEM + self.ndma % (N_DMA_SEM - N_SW_SEM)
                self.ndma += 1
            prev = self.dma_last[si]
            if prev is not None:
                o.deps.add(prev)
            self.dma_cnt[si] += 1
            o.dsem = si
            o.dval = 16 * self.dma_cnt[si]
            self.dma_last[si] = o
            if out:
                self.out_dmas.append(o)
        o.deps.discard(o)
        self.ops[eng].append(o)
        return o

    def finalize_waits(self):
        fin = Op()
        fin.eng = "sp"
        fin.fn = None
        fin.is_dma = False
        fin.idx = len(self.ops["sp"])
        fin.deps = set(self.out_dmas) | set(d for d in self.dma_last if d is not None)
        fin.sig = False
        fin.gidx = self.gcount
        self.ops["sp"].append(fin)
        for e in ENGS:
            for o in self.ops[e]:
                need = []
                for d in o.deps:
                    if d.is_dma:
                        need.append(d)
                    elif d.eng == o.eng and not o.is_dma:
                        if e == "pe":
                            continue
                        need.append(d)
                    else:
                        need.append(d)
                o.deps = need
                for d in need:
                    if not d.is_dma:
                        d.sig = True
        self.nchain = {}
        for e in ENGS:
            r = 0
            for o in self.ops[e]:
                if o.sig and not o.is_dma:
                    o.rank = r
                    r += 1
            self.nchain[e] = (r + SEM_CAP - 1) // SEM_CAP
        for e in ENGS:
            seen = {}
            for o in self.ops[e]:
                ws = {}
                for d in o.deps:
                    if d.is_dma:
                        key = ("dma", d.dsem)
                        val = d.dval
                    else:
                        key = (d.eng, d.rank // SEM_CAP)
                        val = d.rank % SEM_CAP + 1
                    if seen.get(key, 0) >= val:
                        continue
                    if ws.get(key, 0) < val:
                        ws[key] = val
                for k, v in ws.items():
                    seen[k] = v
                o.waits = list(ws.items())

    def emit(self, nc, stack):
        self.finalize_waits()
        sems = {}
        for e in ENGS:
            for c in range(self.nchain[e]):
                sems[(e, c)] = stack.enter_context(nc.semaphore(f"s_{e}_{c}"))
        for i in range(N_DMA_SEM):
            sems[("dma", i)] = stack.enter_context(nc.semaphore(f"s_dma_{i}"))
        block = stack.enter_context(nc.Block())

        def body(e):
            def f(eng):
                for o in self.ops[e]:
                    for k, v in o.waits:
                        eng.wait_ge(sems[k], v)
                    if o.fn is None:
                        continue
                    ins = o.fn(eng)
                    if o.is_dma:
                        ins.then_inc(sems[("dma", o.dsem)], 16)
                    elif o.sig:
                        ins.then_inc(sems[(e, o.rank // SEM_CAP)], 1)
            return f

        block.tensor(body("pe"))
        block.scalar(body("act"))
        block.vector(body("dve"))
        block.gpsimd(body("pool"))
        block.sync(body("sp"))

D = 1024
L = 8192
CTX = 256
DEPTH = 2
NT = 22
NSL = 20
T = 256
ALPHA = (2 * DEPTH) ** 0.25
EPS = 1e-5
NFM = 192
NROW = 128 + 5 * 1024
DFF = 4096

FM_BIN = 0
FM_B1 = 36
FM_BMOD = 68
FM_PS = 116
FM_CB = 118
FM_CG = 120
FM_CBETA = 122
FM_CW = 124
FM_SINK = 186

A_S1W = 0
A_S3W = 8192
A_WPO = 36864
A_WCO = 38912
A_WOUT = 40960
A_WAO = 49152
A_PBD = 53248
A_MODST = 53760
A_MT = 55808
A_XT2 = 57856
A_TT = 59904
A_HT2 = 61952
A_PT = 64000
A_W1 = 0
A_W2 = 32768


def build_program(debug=False, nstages=10**9):
    _bud = [nstages]
    def _go():
        _bud[0] -= 1
        return _bud[0] >= 0
    nc = bass.Bass("TRN2", target_bir_lowering=False)
    dt_in = lambda name, shape: nc.dram_tensor(name, shape, F32, kind="ExternalInput").ap()
    xin = dt_in("xin", [NT * 128, D])
    tab = dt_in("tab", [128, 5, NT * 128])
    kbias_d = dt_in("kbias", [128, 32])
    cvec_d = dt_in("cvec", [128, 16])
    const_d = dt_in("consts", [128, 4, 128])
    fm_d = dt_in("fm", [DEPTH, 128, NFM])
    rowv_d = dt_in("rowv", [DEPTH, 1, NROW])
    w_mod = dt_in("w_mod", [DEPTH, D, 6 * D])
    w_in = dt_in("w_in_r", [DEPTH, D, 36 * 128])
    pbd_d = dt_in("pbd", [DEPTH, 128, 256])
    wpo_d = dt_in("w_pool_out", [DEPTH, 256, D])
    wao_d = dt_in("wao_r", [DEPTH, 128, 4 * D])
    wco_d = dt_in("w_conv_out", [DEPTH, 256, D])
    wout_d = dt_in("w_out", [DEPTH, D, D])
    w1_d = dt_in("w_mlp1", [DEPTH, D, DFF])
    w2_d = dt_in("w_mlp2", [DEPTH, DFF, D])
    out_d = nc.dram_tensor("out", [16 * 128, D], F32, kind="ExternalOutput").ap()
    scr = lambda name, shape, dt=F32: nc.dram_tensor(name, shape, dt, kind="Internal").ap()
    x1D = scr("x1D", [NT * 128, D])
    x2D = scr("x2D", [NT * 128, D])
    hTD = scr("hTD", [128, 8, NT * 128], BF16)
    WUP = NT * 128 + 96
    upD = scr("upD", [128, 2, WUP])
    gluD = scr("gluD", [128, 2, WUP])
    dbg = {}
    if debug:
        dbg["x1"] = nc.dram_tensor("dbg_x1", [NT * 128, D], F32, kind="ExternalOutput").ap()
        dbg["x2"] = nc.dram_tensor("dbg_x2", [NT * 128, D], F32, kind="ExternalOutput").ap()

    S = Sched()
    st = ExitStack()
    with st:
        arena_t = st.enter_context(nc.sbuf_tensor("arena", [128, 65536], BF16))
        NPF = 2048
        pf_t = st.enter_context(nc.sbuf_tensor("pf", [128, NPF], F32))
        NKV = 10 * 128
        NPB = 512 + 2 * NKV
        pb_t = st.enter_context(nc.sbuf_tensor("pb", [128, NPB], BF16))
        bc_t = st.enter_context(nc.sbuf_tensor("bc", [128, 4096], F32))
        NSC = 22528
        sc_t = st.enter_context(nc.sbuf_tensor("scr", [128, NSC], BF16))
        arena_f = arena_t[:, :].bitcast(F32)
        sc_f = sc_t[:, :].bitcast(F32)
        pT = [st.enter_context(nc.psum_tensor(f"pT{i}", [128, 1024], BF16)) for i in range(2)]
        pF = [st.enter_context(nc.psum_tensor(f"pF{i}", [128, 512], F32)) for i in range(6)]

        class FBuf(Buf):
            def __init__(self, fview, tname, off_bf, n_f32):
                Buf.__init__(self, fview, tname, off_bf // 2, n_f32)
                self.off_bf = off_bf
            def keys(self, lo=0, hi=None):
                hi = self.n if hi is None else hi
                g0 = (self.off_bf + 2 * lo) // GRAN
                g1 = (self.off_bf + 2 * hi - 1) // GRAN
                return [(self.tname, g) for g in range(g0, g1 + 1)]

        def AB(off, n):
            return Buf(arena_t, "arena", off, n)

        cur = {"pf": 0, "pb": 0, "scr": 0}
        tens = {"pf": pf_t, "pb": pb_t, "scr": sc_t}
        lim = {"pf": NPF, "pb": NPB, "scr": NSC}

        def alloc(region, n, al=GRAN):
            n_al = (n + al - 1) // al * al
            off = cur[region]
            cur[region] += n_al
            assert cur[region] <= lim[region], (region, cur[region])
            return Buf(tens[region], region, off, n)

        def allocf(n):
            n_al = (2 * n + GRAN - 1) // GRAN * GRAN
            off = cur["scr"]
            cur["scr"] += n_al
            assert cur["scr"] <= NSC, cur["scr"]
            return FBuf(sc_f, "scr", off, n)

        A32 = 32
        identf = alloc("pf", 128); permf = alloc("pf", 128); onesf = alloc("pf", 128); onesLN = alloc("pf", 128)
        fm = [alloc("pf", NFM, A32) for _ in range(DEPTH)]
        modT = [alloc("pf", 96, A32) for _ in range(DEPTH)]
        kbias = alloc("pf", 32, A32); silc = alloc("pf", 16, A32); bvb = alloc("pf", 128, A32); esink = alloc("pf", 8, A32)
        stat = [alloc("pf", 16, A32) for _ in range(4)]
        diag = alloc("pf", 128, A32)
        epsb = alloc("pf", 8, A32)
        identb = alloc("pb", 128); mprev = alloc("pb", 128); mnext = alloc("pb", 128); onesb = alloc("pb", 128)
        kT = alloc("pb", NKV); Vb = alloc("pb", NKV)
        def kvcol(t):
            return (t % 8) * 128 if t < NSL else (8 + t - NSL) * 128
        BC = [Buf(bc_t, "bc", i * 1024, 1024) for i in range(4)]
        a_tab = allocf(768); a_t1 = allocf(256); a_t2 = allocf(256); a_sgb = allocf(256)
        a_upst = allocf(512); a_glust = allocf(512)
        f_tab = allocf(1024); f_upW = allocf(2 * 272); f_gluW = allocf(2 * 288)
        f_reg = allocf(2304)
        a_xn = alloc("scr", 1024); a_hT = alloc("scr", 2048); f_qT = alloc("scr", 1024); f_oT = alloc("scr", 1024)
        f_dT = alloc("scr", 512); f_ypT = alloc("scr", 512); f_sT = alloc("scr", 512)
        R = f_reg
        def sub(b, lo, n):
            return FBuf(sc_f, "scr", b.off_bf + 2 * lo, n)
        a_cst = sub(R, 0, 512)
        q_f = sub(R, 0, 1024); q_t1 = sub(R, 1024, 256); q_t2 = sub(R, 1280, 256)
        at_den = sub(R, 1536, 512)
        p_a = sub(R, 0, 544); p_b = sub(R, 640, 544); p_tmp = sub(R, 1280, 256)
        c_acc0 = sub(R, 0, 512); c_acc1 = sub(R, 512, 512); c_mean = sub(R, 1024, 256); c_var = sub(R, 1280, 256)
        c_z = sub(R, 1536, 512)
        m_gs = sub(R, 0, 768); m_1 = sub(R, 768, 256); m_2 = sub(R, 1024, 256)
        f_mT = AB(A_MT, 2048); f_hT2 = AB(A_HT2, 2048); f_PT = [AB(A_PT + i * 512, 512) for i in range(3)]
        f_xt2 = FBuf(arena_f, "arena", A_XT2, 1024); f_tt = FBuf(arena_f, "arena", A_TT, 1024)
        modst = FBuf(arena_f, "arena", A_MODST, 1024)
        a_xt = BC[3]
        cur["scr"] = 0
        b_xt = allocf(1024); b_pre = allocf(1024); b_tt = allocf(1024)
        b_r = [allocf(256) for _ in range(3)]
        b_xn = alloc("scr", 1024); b_hT = alloc("scr", 2048); b_fT = alloc("scr", 8192)
        W1 = AB(A_W1, 32768); W2 = AB(A_W2, 32768)
        S1W = AB(A_S1W, 8192); S3W = AB(A_S3W, 28672); WPO = AB(A_WPO, 2048); WCO = AB(A_WCO, 2048)
        WOUT = AB(A_WOUT, 8192); WAO = AB(A_WAO, 4096); PBD = AB(A_PBD, 256)

        psk = lambda kind, i, h=None: ("ps", kind, i) if h is None else ("ps", kind, i, h)
        fm_rot = [0]
        FM_BANKS = [0, 1, 4, 5]
        def fm_slot():
            i = FM_BANKS[fm_rot[0] % 4]
            fm_rot[0] += 1
            return pF[i][:, 0:256], psk("F", i)
        big_rot = [0]
        def big_slot():
            i = 2 + big_rot[0] % 2
            big_rot[0] += 1
            return pF[i], psk("F", i)
        st_rot = [0]
        def stat_buf():
            st_rot[0] += 1
            return stat[st_rot[0] % 4]

        def dma(q, out, in_, r=(), w=(), is_out=False):
            S.add(q, lambda e: e.dma_start(out=out, in_=in_), r=r, w=w, dma=True, out=is_out)

        def act(out, in_, func, r, w, bias=0.0, scale=1.0):
            S.add("act", lambda e: e.activation(out=out, in_=in_, func=func, bias=bias, scale=scale), r=r, w=w)

        def tt(eng, out, in0, in1, op, r, w):
            S.add(eng, lambda e: e.tensor_tensor(out=out, in0=in0, in1=in1, op=op), r=r, w=w)

        def stt(eng, out, in0, scalar, in1, op0, op1, r, w):
            S.add(eng, lambda e: e.scalar_tensor_tensor(out=out, in0=in0, scalar=scalar, in1=in1, op0=op0, op1=op1),
                  r=r, w=w)

        def ts(eng, out, in0, s1, s2, op0, op1, r, w):
            S.add(eng, lambda e: e.tensor_scalar(out=out, in0=in0, scalar1=s1, scalar2=s2, op0=op0, op1=op1), r=r, w=w)

        def ts1(eng, out, in_, scalar, op, r, w):
            S.add(eng, lambda e: e.tensor_single_scalar(out=out, in_=in_, scalar=scalar, op=op), r=r, w=w)

        def mmK(out, pairs, r, w):
            n = len(pairs)
            def f(e):
                ins = None
                for i, (l_, r_) in enumerate(pairs):
                    ins = e.matmul(out, lhsT=l_, rhs=r_, start=(i == 0), stop=(i == n - 1))
                return ins
            S.add("pe", f, r=r, w=w)

        def layer_norm_rows(src, dst_ap, srcbuf_keys, dst_keys):
            sb_ = stat_buf()
            S.add("dve", lambda e: e.bn_stats(out=sb_.ap(0, 6), in_=src.ap(0, 512)), r=srcbuf_keys, w=[(sb_, 0, 6)])
            S.add("dve", lambda e: e.bn_stats(out=sb_.ap(6, 12), in_=src.ap(512, 1024)), r=srcbuf_keys, w=[(sb_, 6, 12)])
            S.add("dve", lambda e: e.bn_aggr(out=sb_.ap(12, 14), in_=sb_.ap(0, 12).rearrange("p (a b) -> p a b", b=3)),
                  r=[sb_], w=[sb_])
            act(sb_.ap(14, 15), sb_.ap(13, 14), AF.Sqrt, r=[sb_], w=[sb_], bias=eps_ap(), scale=1.0)
            S.add("dve", lambda e: e.reciprocal(out=sb_.ap(14, 15), in_=sb_.ap(14, 15)), r=[sb_], w=[sb_])
            ts("dve", sb_.ap(15, 16), sb_.ap(12, 13), sb_.ap(14, 15), -1.0, ALU.mult, ALU.mult, r=[sb_], w=[sb_])
            act(dst_ap, src.ap(), AF.Identity, r=srcbuf_keys + [sb_], w=dst_keys, bias=sb_.ap(15, 16), scale=sb_.ap(14, 15))

        def eps_ap():
            return epsb.ap(0, 1)

        S.add("dve", lambda e: e.memset(epsb.ap(), EPS), w=[epsb])
        S.add("dve", lambda e: e.memset(onesf.ap(), 1.0), w=[onesf])
        S.add("dve", lambda e: e.memset(onesLN.ap(), 1.0 / 256.0), w=[onesLN])
        S.add("dve", lambda e: e.memset(onesb.ap(), 1.0), w=[onesb])
        dma("sp", identf.ap(), const_d[:, 0, :], w=[identf])
        dma("sp", permf.ap(), const_d[:, 1, :], w=[permf])
        dma("pool", mprev.ap(), const_d[:, 2, :], w=[mprev])
        dma("pool", mnext.ap(), const_d[:, 3, :], w=[mnext])
        if os.environ.get('IDACT'):
            act(identb.ap(), identf.ap(), AF.Copy, r=[identf], w=[identb])
        else:
            dma("pool", identb.ap(), const_d[:, 0, :], w=[identb])
        dma("sp", kbias.ap(), kbias_d, w=[kbias])
        dma("sp", silc.ap(), cvec_d, w=[silc])
        for l in range(DEPTH):
            dma("sp", fm[l].ap(), fm_d[l], w=[fm[l]])
        act(silc.ap(), silc.ap(), AF.Silu, r=[silc], w=[silc])
        S.add("dve", lambda e: e.memset(a_cst.ap(), 0.0), w=[a_cst])
        ZC0 = 32
        CC0 = 32 + NSL * 128 + 32
        def colof(tile):
            return ZC0 + tile * 128 if tile < NSL else CC0 + (tile - NSL) * 128
        for dd, nm in ((upD, "upD"), (gluD, "gluD")):
            for c0 in (0, ZC0 + NSL * 128, CC0 + 256):
                w_ = 32 if c0 + 32 <= WUP else WUP - c0
                if w_ <= 0:
                    continue
                dma("sp", dd[:, :, c0:c0 + w_], a_cst.v("p (c n) -> p c n", c=2)[:, :, 0:w_], r=[a_cst], w=[(nm, "z", c0)])

        def load_phaseA_weights(l):
            q = "pool"
            for kc in range(8):
                dma(q, S1W.ap(kc * 1024, (kc + 1) * 1024), w_in[l, kc * 128:(kc + 1) * 128, 0:1024], w=[(S1W, kc * 1024, (kc + 1) * 1024)])
            for kc in range(8):
                dma(q, S3W.ap(kc * 3584, (kc + 1) * 3584), w_in[l, kc * 128:(kc + 1) * 128, 1024:4608], w=[(S3W, kc * 3584, (kc + 1) * 3584)])
            dma(q, PBD.ap(), pbd_d[l], w=[PBD])
            dma(q, WPO.v("p (k n) -> p k n", k=2), wpo_d[l].rearrange("(k p) n -> p k n", p=128), w=[WPO])
            dma(q, WCO.v("p (k n) -> p k n", k=2), wco_d[l].rearrange("(k p) n -> p k n", p=128), w=[WCO])
            dma(q, WAO.ap(), wao_d[l], w=[WAO])
            dma(q, WOUT.v("p (k n) -> p k n", k=8), wout_d[l].rearrange("(k p) n -> p k n", p=128), w=[WOUT])

        def load_phaseB_weights(l):
            q = "pool"
            for kc in range(8):
                dma(q, W1.ap(kc * 4096, (kc + 1) * 4096), w1_d[l, kc * 128:(kc + 1) * 128, :], w=[(W1, kc * 4096, (kc + 1) * 4096)])
            for g in range(4):
                dma(q, W2.v("p (c n) -> p c n", n=1024)[:, g * 8:(g + 1) * 8, :],
                    w2_d[l, g * 1024:(g + 1) * 1024, :].rearrange("(c p) n -> p c n", p=128),
                    w=[(W2, g * 8192, (g + 1) * 8192)])

        def mod_gen(l):
            def D_(oc):
                dma("sp", modst.v("p (k n) -> p k n", k=8), w_mod[l, :, oc * 128:(oc + 1) * 128].rearrange("(k p) n -> p k n", p=128),
                    w=[modst])
            def M_(oc):
                o_ap, o_k = fm_slot()
                mmK(o_ap[:, 0:2], [(modst.ap(kc * 128, (kc + 1) * 128), silc.ap(kc * 2, kc * 2 + 2)) for kc in range(8)],
                    r=[modst, silc], w=[o_k])
                addc = 1.0 if (oc // 8) in (1, 4) else 0.0
                ts("dve", modT[l].ap(oc * 2, oc * 2 + 2), o_ap[:, 0:2], fm[l].ap(FM_BMOD + oc, FM_BMOD + oc + 1), addc,
                   ALU.add, ALU.add, r=[o_k, fm[l]], w=[(modT[l], oc * 2, oc * 2 + 2)])
            D_(0)
            yield
            for oc in range(48):
                M_(oc)
                if oc + 1 < 48:
                    D_(oc + 1)
                yield
        mod_pull = [None]

        def bcast_from_modT(l, vec_idx, j, dstbuf):
            for c in range(8):
                oc = vec_idx * 8 + c
                ts1("dve", diag.ap(), identf.ap(), modT[l].ap(oc * 2 + j, oc * 2 + j + 1), ALU.mult,
                   r=[identf, (modT[l], oc * 2, oc * 2 + 2)], w=[diag])
                o_ap, o_k = fm_slot()
                mmK(o_ap[:, 0:128], [(onesf.ap(), diag.ap())], r=[onesf, diag], w=[o_k])
                act(dstbuf.ap(c * 128, (c + 1) * 128), o_ap[:, 0:128], AF.Copy, r=[o_k], w=[(dstbuf, c * 128, (c + 1) * 128)])

        def row_bcast(l, off, n, dst_ap, wkeys):
            dma("sp", dst_ap, rowv_d[l, :, off:off + n].broadcast_to([128, n]), w=wkeys)

        def stage_S1(l, t0, xsrc, j):
            if not _go():
                return
            c0 = t0 * 128
            dma("sp", a_tab.v("p (a n) -> p a n", a=3), tab[:, 0:3, c0:c0 + T], w=[a_tab])
            if _SUB < 1:
                return
            for ti in range(2):
                t = t0 + ti
                dma("sp", a_xt.ap(), xsrc[t * 128:(t + 1) * 128, :], r=[("x", id(xsrc), t)], w=[a_xt])
                if _SUB2 < 1:
                    continue
                layer_norm_rows(a_xt, a_xn.ap(), [a_xt], [a_xn])
                if _SUB2 < 2:
                    continue
                pt = pT[ti]
                def trs(e, pt=pt):
                    ins = None
                    for c in range(int(os.environ.get('NTR', '8'))):
                        ins = e.transpose(pt[:, c * 128:(c + 1) * 128], a_xn.ap(c * 128, (c + 1) * 128), identb.ap())
                    return ins
                S.add("pe", trs, r=[a_xn, identb], w=[psk("T", ti)])
                if _SUB2 < 3:
                    continue
                for c in range(8):
                    if _SUB2 < 4 and c % 2 == 1:
                        continue
                    sc = modT[l].ap((8 + c) * 2 + j, (8 + c) * 2 + j + 1)
                    sh = modT[l].ap((0 + c) * 2 + j, (0 + c) * 2 + j + 1)
                    dst = a_hT.ap(c * 256 + ti * 128, c * 256 + ti * 128 + 128)
                    wk = [(a_hT, c * 256 + ti * 128, c * 256 + ti * 128 + 128)]
                    if True:
                        act(dst, pt[:, c * 128:(c + 1) * 128], AF.Identity, r=[psk("T", ti), modT[l]], w=wk, bias=sh, scale=sc)
                    else:
                        ts("dve", dst, pt[:, c * 128:(c + 1) * 128], sc, sh, ALU.mult, ALU.add, r=[psk("T", ti), modT[l]], w=wk)
            if _SUB < 2:
                return
            dma("sp", hTD[:, :, c0:c0 + T], a_hT.v("p (c n) -> p c n", c=8), r=[a_hT], w=[("hTD", t0), ("hTD", t0 + 1)])
            hk = lambda kc: a_hT.ap(kc * 256, kc * 256 + 256)
            wS1 = lambda kc, n: S1W.ap(kc * 1024 + n * 128, kc * 1024 + n * 128 + 128)
            wS1k = lambda n: [(S1W, kc * 1024 + n * 128, kc * 1024 + n * 128 + 128) for kc in range(8)]
            bfm = lambda n: fm[l].ap(FM_BIN + n, FM_BIN + n + 1)
            cosA = a_tab.ap(0, 256); sinA = a_tab.ap(256, 512); vmA = a_tab.ap(512, 768)
            if _SUB < 3:
                return
            o_ap, o_k = fm_slot()
            mmK(o_ap, [(wS1(kc, 0), hk(kc)) for kc in range(8)], r=[a_hT] + wS1k(0), w=[o_k])
            act(a_t1.ap(), o_ap, AF.Identity, r=[o_k, fm[l]], w=[a_t1], bias=bfm(0))
            o2, o2k = fm_slot()
            mmK(o2, [(permf.ap(), a_t1.ap())], r=[permf, a_t1], w=[o2k])
            tt("dve", a_t2.ap(), o2, sinA, ALU.mult, r=[o2k, a_tab], w=[a_t2])
            tt("dve", a_t1.ap(), a_t1.ap(), cosA, ALU.mult, r=[a_t1, a_tab], w=[a_t1])
            for ti in range(2):
                kc_ = kvcol(t0 + ti)
                tt("dve", kT.ap(kc_, kc_ + 128), a_t1.ap(ti * 128, ti * 128 + 128), a_t2.ap(ti * 128, ti * 128 + 128), ALU.add,
                   r=[a_t1, a_t2], w=[(kT, kc_, kc_ + 128)])
            if _SUB < 4:
                return
            for ti in range(2):
                t = t0 + ti
                o_ap, o_k = fm_slot()
                mmK(o_ap[:, 0:128], [(a_hT.ap(kc * 256 + ti * 128, kc * 256 + ti * 128 + 128), wS1(kc, 1)) for kc in range(8)],
                    r=[a_hT] + wS1k(1), w=[o_k])
                tt("dve", Vb.ap(kvcol(t), kvcol(t) + 128), o_ap[:, 0:128], bvb.ap(), ALU.add, r=[o_k, bvb], w=[(Vb, kvcol(t), kvcol(t) + 128)])
            if _SUB < 5:
                return
            for c in range(2):
                o_ap, o_k = fm_slot()
                mmK(o_ap, [(wS1(kc, 2 + c), hk(kc)) for kc in range(8)], r=[a_hT] + wS1k(2 + c), w=[o_k])
                stt("dve", a_upst.ap(c * 256, (c + 1) * 256), o_ap, bfm(2 + c), vmA, ALU.add, ALU.mult,
                    r=[o_k, fm[l], a_tab], w=[(a_upst, c * 256, (c + 1) * 256)])
            col = colof(t0)
            dma("sp", upD[:, :, col:col + T], a_upst.v("p (c n) -> p c n", c=2), r=[a_upst], w=[("upD", t0), ("upD", t0 + 1)])
            if _SUB < 6:
                return
            for c in range(2):
                ob, obk = fm_slot()
                mmK(ob, [(wS1(kc, 6 + c), hk(kc)) for kc in range(8)], r=[a_hT] + wS1k(6 + c), w=[obk])
                act(a_sgb.ap(), ob, AF.Sigmoid, r=[obk, fm[l]], w=[a_sgb], bias=bfm(6 + c))
                tt("dve", a_sgb.ap(), a_sgb.ap(), vmA, ALU.mult, r=[a_sgb, a_tab], w=[a_sgb])
                oa, oak = fm_slot()
                mmK(oa, [(wS1(kc, 4 + c), hk(kc)) for kc in range(8)], r=[a_hT] + wS1k(4 + c), w=[oak])
                stt("dve", a_glust.ap(c * 256, (c + 1) * 256), oa, bfm(4 + c), a_sgb.ap(), ALU.add, ALU.mult,
                    r=[oak, fm[l], a_sgb], w=[(a_glust, c * 256, (c + 1) * 256)])
            dma("sp", gluD[:, :, col:col + T], a_glust.v("p (c n) -> p c n", c=2), r=[a_glust], w=[("gluD", t0), ("gluD", t0 + 1)])

        def stage_FULL(l, t0, xsrc, xdst, is_ctx):
            if not _go():
                return
            c0 = t0 * 128
            col = colof(t0)
            nb = [t0 - 1, t0, t0 + 1, t0 + 2]
            nbk = lambda nm: [(nm, t) for t in nb] + [(nm, "z", 0), (nm, "z", ZC0 + NSL * 128), (nm, "z", CC0 + 256)]
            dma("sp", f_hT2.v("p (c n) -> p c n", c=8), hTD[:, :, c0:c0 + T], r=[("hTD", t0), ("hTD", t0 + 1)], w=[f_hT2])
            dma("sp", f_tab.v("p (a n) -> p a n", a=4)[:, 0:2, :], tab[:, 0:2, c0:c0 + T], w=[(f_tab, 0, 512)])
            dma("sp", f_tab.v("p (a n) -> p a n", a=4)[:, 2:4, :], tab[:, 3:5, c0:c0 + T], w=[(f_tab, 512, 1024)])
            dma("sp", f_upW.v("p (c n) -> p c n", c=2), upD[:, :, col - 8:col + 264], r=nbk("upD"), w=[f_upW])
            dma("sp", f_gluW.v("p (c n) -> p c n", c=2)[:, :, 0:286], gluD[:, :, col - 15:col + 271], r=nbk("gluD"), w=[f_gluW])
            h2 = lambda kc: f_hT2.ap(kc * 256, kc * 256 + 256)
            wS3 = lambda kc, n: S3W.ap(kc * 3584 + n * 128, kc * 3584 + n * 128 + 128)
            wS3k = lambda n: [(S3W, kc * 3584 + n * 128, kc * 3584 + n * 128 + 128) for kc in range(8)]
            bfm = lambda n: fm[l].ap(FM_BIN + n, FM_BIN + n + 1)
            cosF = f_tab.ap(0, 256); sinF = f_tab.ap(256, 512)
            for jq in range(4):
                o_ap, o_k = fm_slot()
                mmK(o_ap, [(wS3(kc, jq), h2(kc)) for kc in range(8)], r=[f_hT2] + wS3k(jq), w=[o_k])
                qf = q_f.ap(jq * 256, (jq + 1) * 256)
                qfk = [(q_f, jq * 256, (jq + 1) * 256)]
                act(qf, o_ap, AF.Identity, r=[o_k, fm[l]], w=qfk, bias=bfm(8 + jq))
                o2, o2k = fm_slot()
                mmK(o2, [(permf.ap(), qf)], r=[permf] + qfk, w=[o2k])
                tt("dve", q_t2.ap(), o2, sinF, ALU.mult, r=[o2k, f_tab], w=[q_t2])
                tt("dve", q_t1.ap(), qf, cosF, ALU.mult, r=qfk + [f_tab], w=[q_t1])
                tt("dve", f_qT.ap(jq * 256, (jq + 1) * 256), q_t1.ap(), q_t2.ap(), ALU.add, r=[q_t1, q_t2],
                   w=[(f_qT, jq * 256, (jq + 1) * 256)])
            u3 = f_upW.v("p (c n) -> p c n", c=2)
            pa3 = p_a.v("p (c n) -> p c n", c=2)
            pb3 = p_b.v("p (c n) -> p c n", c=2)
            pinv = lambda c, hp0, hp1: f_tab.ap(512 + c * 256, 512 + (c + 1) * 256, hp0, hp1)
            def pool_d(src3, c, half):
                hp0, hp1 = half * 64, half * 64 + 64
                tt("dve", p_tmp.ap(0, 256, hp0, hp1), src3[hp0:hp1, c, 8:264], pinv(c, hp0, hp1), ALU.mult,
                   r=[p_a, p_b, f_tab], w=[p_tmp])
                tt("dve", f_dT.ap(c * 256, (c + 1) * 256, hp0, hp1), p_tmp.ap(0, 256, hp0, hp1), u3[hp0:hp1, c, 8:264],
                   ALU.subtract, r=[p_tmp, f_upW], w=[f_dT])
            tt("dve", pa3[:, :, 1:272], u3[:, :, 0:271], u3[:, :, 1:272], ALU.add, r=[f_upW], w=[p_a])
            pool_d(pa3, 0, 0)
            tt("dve", pb3[:, :, 2:271], pa3[:, :, 1:270], pa3[:, :, 3:272], ALU.add, r=[p_a], w=[p_b])
            pool_d(pb3, 0, 1)
            tt("dve", pa3[:, :, 4:269], pb3[:, :, 2:267], pb3[:, :, 6:271], ALU.add, r=[p_b], w=[p_a])
            pool_d(pa3, 1, 0)
            tt("dve", pb3[:, :, 8:265], pa3[:, :, 4:261], pa3[:, :, 12:269], ALU.add, r=[p_a], w=[p_b])
            pool_d(pb3, 1, 1)
            for c in range(2):
                o_ap, o_k = fm_slot()
                mmK(o_ap, [(PBD.ap(c * 128, (c + 1) * 128), f_dT.ap(c * 256, (c + 1) * 256))], r=[PBD, f_dT], w=[o_k])
                act(f_ypT.ap(c * 256, (c + 1) * 256), o_ap, AF.Identity, r=[o_k, fm[l]], w=[(f_ypT, c * 256, (c + 1) * 256)],
                    scale=fm[l].ap(FM_PS + c, FM_PS + c + 1))
            g3 = f_gluW.v("p (c n) -> p c n", c=2)
            def conv_gen():
                for c in range(2):
                    cw = lambda jj: fm[l].ap(FM_CW + c * 31 + jj, FM_CW + c * 31 + jj + 1)
                    a0 = c_acc0.ap(c * 256, (c + 1) * 256); a1 = c_acc1.ap(c * 256, (c + 1) * 256)
                    k0 = [(c_acc0, c * 256, (c + 1) * 256)]; k1 = [(c_acc1, c * 256, (c + 1) * 256)]
                    ts("dve", a0, g3[:, c, 0:256], cw(0), fm[l].ap(FM_CB + c, FM_CB + c + 1), ALU.mult, ALU.add, r=[f_gluW, fm[l]], w=k0)
                    yield
                    ts1("dve", a1, g3[:, c, 1:257], cw(1), ALU.mult, r=[f_gluW, fm[l]], w=k1)
                    yield
                    for jj in range(2, 31):
                        aa, kk = (a0, k0) if jj % 2 == 0 else (a1, k1)
                        stt("dve", aa, g3[:, c, jj:jj + 256], cw(jj), aa, ALU.mult, ALU.add, r=[f_gluW, fm[l]] + kk, w=kk)
                        yield
                    tt("dve", a0, a0, a1, ALU.add, r=k0 + k1, w=k0)
                    yield
                    act(a1, a0, AF.Square, r=k0, w=k1)
                    yield
            cg = conv_gen()
            pti = [0]
            for ti in range(2):
                t = t0 + ti
                if is_ctx:
                    klist = [(NSL, None), (NSL + 1, None)]
                else:
                    klist = [(t - 1, mprev), (t, None), (t + 1, mnext), (NSL, None), (NSL + 1, None)]
                for h in range(2):
                    po, pok = pF[4], psk("F", 4)
                    pd, pdk = pF[5], psk("F", 5)
                    hp0, hp1 = h * 64, h * 64 + 64
                    qv = f_qT.v("p (g n) -> p g n", g=4, p0=hp0, p1=hp1)[:, :, ti * 128:(ti + 1) * 128]
                    for ki, (kt, msk) in enumerate(klist):
                        s_ap, s_k = big_slot()
                        mmK(s_ap[:, :].rearrange("p (g n) -> p g n", g=4),
                            [(kT.ap(kvcol(kt), kvcol(kt) + 128, hp0, hp1), qv)],
                            r=[(kT, kvcol(kt), kvcol(kt) + 128), f_qT], w=[s_k])
                        P = f_PT[pti[0] % 3]
                        pti[0] += 1
                        act(P.ap(), s_ap[:, :], AF.Exp, r=[s_k, kbias], w=[P], bias=kbias.ap(kt, kt + 1), scale=0.125)
                        if msk is not None:
                            tt("dve", P.v("p (g n) -> p g n", g=4), P.v("p (g n) -> p g n", g=4),
                               msk.ap().unsqueeze(1).broadcast_to([128, 4, 128]), ALU.mult, r=[P, msk], w=[P])
                        first = ki == 0
                        last = ki == len(klist) - 1
                        S.add("pe", lambda e, kt=kt, P=P, first=first, last=last, po=po: e.matmul(
                            po[:, :], lhsT=Vb.ap(kvcol(kt), kvcol(kt) + 128), rhs=P.ap(), start=first, stop=last),
                            r=[(Vb, kvcol(kt), kvcol(kt) + 128), P], w=[pok])
                        S.add("pe", lambda e, P=P, first=first, last=last, pd=pd: e.matmul(
                            pd[:, :], lhsT=onesb.ap(), rhs=P.ap(), start=first, stop=last),
                            r=[onesb, P], w=[pdk])
                        for _ in range(4):
                            next(cg, None)
                    den = at_den.v("p (g n) -> p g n", g=4, p0=hp0, p1=hp1)
                    tt("dve", den, pd[hp0:hp1, :].rearrange("p (g n) -> p g n", g=4),
                       esink.ap(0, 4, hp0, hp1).unsqueeze(2).broadcast_to([64, 4, 128]), ALU.add, r=[pdk, esink], w=[at_den])
                    S.add("dve", lambda e, den=den: e.reciprocal(out=den, in_=den), r=[at_den], w=[at_den])
                    tt("dve", f_oT.v("p (g n) -> p g n", g=4, p0=hp0, p1=hp1)[:, :, ti * 128:(ti + 1) * 128],
                       po[hp0:hp1, :].rearrange("p (g n) -> p g n", g=4), den, ALU.mult, r=[pok, at_den], w=[f_oT])
            for _ in cg:
                pass
            pm_, pmk = fm_slot()
            mmK(pm_, [(onesLN.ap(), c_acc0.ap(c * 256, (c + 1) * 256)) for c in range(2)], r=[onesLN, c_acc0], w=[pmk])
            pe_, pek = fm_slot()
            mmK(pe_, [(onesLN.ap(), c_acc1.ap(c * 256, (c + 1) * 256)) for c in range(2)], r=[onesLN, c_acc1], w=[pek])
            act(c_mean.ap(), pm_, AF.Copy, r=[pmk], w=[c_mean])
            tt("dve", c_var.ap(), c_mean.ap(), c_mean.ap(), ALU.mult, r=[c_mean], w=[c_var])
            tt("dve", c_var.ap(), pe_, c_var.ap(), ALU.subtract, r=[pek, c_var], w=[c_var])
            ts1("dve", c_var.ap(), c_var.ap(), 0.0, ALU.max, r=[c_var], w=[c_var])
            act(c_var.ap(), c_var.ap(), AF.Sqrt, r=[c_var], w=[c_var], bias=eps_ap())
            S.add("dve", lambda e: e.reciprocal(out=c_var.ap(), in_=c_var.ap()), r=[c_var], w=[c_var])
            for c in range(2):
                zc = c_z.ap(c * 256, (c + 1) * 256); zk = [(c_z, c * 256, (c + 1) * 256)]
                tt("dve", zc, c_acc0.ap(c * 256, (c + 1) * 256), c_mean.ap(), ALU.subtract, r=[c_acc0, c_mean], w=zk)
                tt("dve", zc, zc, c_var.ap(), ALU.mult, r=zk + [c_var], w=zk)
                act(f_sT.ap(c * 256, (c + 1) * 256), zc, AF.Silu, r=zk + [fm[l]], w=[(f_sT, c * 256, (c + 1) * 256)],
                    bias=fm[l].ap(FM_CBETA + c, FM_CBETA + c + 1), scale=fm[l].ap(FM_CG + c, FM_CG + c + 1))
            for c in range(8):
                if mod_pull[0] is not None:
                    next(mod_pull[0], None)
                gsb = sub(R, (c % 2) * 768, 768)
                gv = lambda b_: gsb.ap(b_ * 256, (b_ + 1) * 256)
                gk = lambda b_: [(gsb, b_ * 256, (b_ + 1) * 256)]
                for b in range(3):
                    n = 4 + c * 3 + b
                    o_ap, o_k = fm_slot()
                    mmK(o_ap, [(wS3(kc, n), h2(kc)) for kc in range(8)], r=[f_hT2] + wS3k(n), w=[o_k])
                    act(gv(b), o_ap, AF.Sigmoid, r=[o_k, fm[l]], w=gk(b), bias=bfm(8 + n))
                oA, oAk = fm_slot()
                mmK(oA, [(WPO.ap(kc * 1024 + c * 128, kc * 1024 + c * 128 + 128), f_ypT.ap(kc * 256, (kc + 1) * 256)) for kc in range(2)],
                    r=[WPO, f_ypT], w=[oAk])
                tt("dve", gv(0), oA, gv(0), ALU.mult, r=[oAk] + gk(0), w=gk(0))
                oB, oBk = fm_slot()
                mmK(oB, [(WAO.ap(g * 1024 + c * 128, g * 1024 + c * 128 + 128), f_oT.ap(g * 256, (g + 1) * 256)) for g in range(4)],
                    r=[WAO, f_oT], w=[oBk])
                tt("dve", gv(1), oB, gv(1), ALU.mult, r=[oBk] + gk(1), w=gk(1))
                tt("dve", gv(0), gv(0), gv(1), ALU.add, r=gk(0) + gk(1), w=gk(0))
                oC, oCk = fm_slot()
                mmK(oC, [(WCO.ap(kc * 1024 + c * 128, kc * 1024 + c * 128 + 128), f_sT.ap(kc * 256, (kc + 1) * 256)) for kc in range(2)],
                    r=[WCO, f_sT], w=[oCk])
                tt("dve", gv(2), oC, gv(2), ALU.mult, r=[oCk] + gk(2), w=gk(2))
                tt("dve", f_mT.ap(c * 256, (c + 1) * 256), gv(0), gv(2), ALU.add, r=gk(0) + gk(2), w=[(f_mT, c * 256, (c + 1) * 256)])
            for ti in range(2):
                t = t0 + ti
                dma("sp", f_xt2.ap(), xsrc[t * 128:(t + 1) * 128, :], r=[("x", id(xsrc), t)], w=[f_xt2])
                for hf in range(2):
                    y_ap, y_k = big_slot()
                    mmK(y_ap[:, :], [(f_mT.ap(kc * 256 + ti * 128, kc * 256 + ti * 128 + 128),
                                      WOUT.ap(kc * 1024 + hf * 512, kc * 1024 + hf * 512 + 512)) for kc in range(8)],
                        r=[f_mT, WOUT], w=[y_k])
                    tt("dve", f_tt.ap(hf * 512, (hf + 1) * 512), y_ap[:, :], BC[0].ap(hf * 512, (hf + 1) * 512), ALU.mult,
                       r=[y_k, BC[0]], w=[(f_tt, hf * 512, (hf + 1) * 512)])
                stt("dve", f_tt.ap(), f_xt2.ap(), ALPHA, f_tt.ap(), ALU.mult, ALU.add, r=[f_xt2, f_tt], w=[f_tt])
                layer_norm_rows(f_tt, f_tt.ap(), [f_tt], [f_tt])
                tt("dve", f_tt.ap(), f_tt.ap(), BC[1].ap(), ALU.mult, r=[f_tt, BC[1]], w=[f_tt])
                tt("dve", f_tt.ap(), f_tt.ap(), BC[2].ap(), ALU.add, r=[f_tt, BC[2]], w=[f_tt])
                dma("sp", xdst[t * 128:(t + 1) * 128, :], f_tt.ap(), r=[f_tt], w=[("x", id(xdst), t)])
                if debug and l == 0:
                    dma("sp", dbg["x1"][t * 128:(t + 1) * 128, :], f_tt.ap(), r=[f_tt], is_out=True)

        def stage_B(l, t0, xsrc, xdst, j, out_rows=None):
            if not _go():
                return
            for ti in range(2):
                t = t0 + ti
                dma("sp", b_xt.ap(), xsrc[t * 128:(t + 1) * 128, :], r=[("x", id(xsrc), t)], w=[b_xt])
                layer_norm_rows(b_xt, b_xn.ap(), [b_xt], [b_xn])
                pt = pT[ti]
                def trs(e, pt=pt):
                    ins = None
                    for c in range(8):
                        ins = e.transpose(pt[:, c * 128:(c + 1) * 128], b_xn.ap(c * 128, (c + 1) * 128), identb.ap())
                    return ins
                S.add("pe", trs, r=[b_xn, identb], w=[psk("T", ti)])
                for c in range(8):
                    sc = modT[l].ap((32 + c) * 2 + j, (32 + c) * 2 + j + 1)
                    sh = modT[l].ap((24 + c) * 2 + j, (24 + c) * 2 + j + 1)
                    dst = b_hT.ap(c * 256 + ti * 128, c * 256 + ti * 128 + 128)
                    wk = [(b_hT, c * 256 + ti * 128, c * 256 + ti * 128 + 128)]
                    if True:
                        act(dst, pt[:, c * 128:(c + 1) * 128], AF.Identity, r=[psk("T", ti), modT[l]], w=wk, bias=sh, scale=sc)
                    else:
                        ts("dve", dst, pt[:, c * 128:(c + 1) * 128], sc, sh, ALU.mult, ALU.add, r=[psk("T", ti), modT[l]], w=wk)
                if ti == 0:
                    stt("dve", b_pre.ap(), b_xt.ap(), ALPHA, BC[1].ap(), ALU.mult, ALU.add, r=[b_xt, BC[1]], w=[b_pre])
            for c in range(32):
                o_ap, o_k = fm_slot()
                mmK(o_ap, [(W1.ap(kc * 4096 + c * 128, kc * 4096 + c * 128 + 128), b_hT.ap(kc * 256, (kc + 1) * 256)) for kc in range(8)],
                    r=[b_hT] + [(W1, kc * 4096 + c * 128, kc * 4096 + c * 128 + 128) for kc in range(8)], w=[o_k])
                rb = b_r[c % 3]
                act(rb.ap(), o_ap, AF.Relu, r=[o_k, fm[l]], w=[rb], bias=fm[l].ap(FM_B1 + c, FM_B1 + c + 1))
                tt("dve", b_fT.ap(c * 256, (c + 1) * 256), rb.ap(), rb.ap(), ALU.mult, r=[rb], w=[(b_fT, c * 256, (c + 1) * 256)])
            for ti in range(2):
                t = t0 + ti
                if ti == 1:
                    stt("dve", b_pre.ap(), b_xt.ap(), ALPHA, BC[1].ap(), ALU.mult, ALU.add, r=[b_xt, BC[1]], w=[b_pre])
                for hf in range(2):
                    y_ap, y_k = pF[2 + hf], psk("F", 2 + hf)
                    mmK(y_ap[:, :], [(b_fT.ap(c * 256 + ti * 128, c * 256 + ti * 128 + 128),
                                      W2.ap(c * 1024 + hf * 512, c * 1024 + hf * 512 + 512)) for c in range(32)],
                        r=[b_fT, W2], w=[y_k])
                    tt("dve", b_tt.ap(hf * 512, (hf + 1) * 512), y_ap[:, :], BC[0].ap(hf * 512, (hf + 1) * 512), ALU.mult,
                       r=[y_k, BC[0]], w=[(b_tt, hf * 512, (hf + 1) * 512)])
                tt("dve", b_tt.ap(), b_tt.ap(), b_pre.ap(), ALU.add, r=[b_tt, b_pre], w=[b_tt])
                layer_norm_rows(b_tt, b_tt.ap(), [b_tt], [b_tt])
                tt("dve", b_tt.ap(), b_tt.ap(), BC[2].ap(), ALU.mult, r=[b_tt, BC[2]], w=[b_tt])
                tt("dve", b_tt.ap(), b_tt.ap(), BC[3].ap(), ALU.add, r=[b_tt, BC[3]], w=[b_tt])
                if out_rows is not None:
                    r0 = out_rows + ti * 128
                    dma("sp", out_d[r0:r0 + 128, :], b_tt.ap(), r=[b_tt], is_out=True)
                else:
                    dma("sp", xdst[t * 128:(t + 1) * 128, :], b_tt.ap(), r=[b_tt], w=[("x", id(xdst), t)])
                    if debug:
                        dma("sp", dbg["x2"][t * 128:(t + 1) * 128, :], b_tt.ap(), r=[b_tt], is_out=True)

        load_phaseA_weights(0)
        for _ in mod_gen(0):
            pass
        xcur = xin
        for l in range(DEPTH):
            last = l == DEPTH - 1
            row_bcast(l, 0, 128, bvb.ap(), [bvb])
            act(esink.ap(0, 4), fm[l].ap(FM_SINK, FM_SINK + 4), AF.Exp, r=[fm[l]], w=[esink])
            row_bcast(l, 128, 1024, BC[1].ap(), [BC[1]])
            row_bcast(l, 128 + 1024, 1024, BC[2].ap(), [BC[2]])
            mix_lo, mix_hi = (0, 20) if l == 0 else (1, 19)
            if not last:
                mod_pull[0] = mod_gen(l + 1)
            stage_S1(l, NSL, xcur, 1)
            if not last:
                bcast_from_modT(l, 2, 1, BC[0])
                stage_FULL(l, NSL, xcur, x1D, True)
            bcast_from_modT(l, 2, 0, BC[0])
            s1_pairs = list(range(mix_lo, mix_hi, 2))
            full_pairs = list(range(mix_lo + 1, mix_hi - 1, 2))
            stage_S1(l, s1_pairs[0], xcur, 0)
            for i, fp in enumerate(full_pairs):
                stage_S1(l, s1_pairs[i + 1], xcur, 0)
                stage_FULL(l, fp, xcur, x1D, False)
            if mod_pull[0] is not None:
                for _ in mod_pull[0]:
                    pass
                mod_pull[0] = None
            load_phaseB_weights(l)
            row_bcast(l, 128 + 3 * 1024, 1024, BC[2].ap(), [BC[2]])
            row_bcast(l, 128 + 4 * 1024, 1024, BC[3].ap(), [BC[3]])
            def prep_B(j):
                bcast_from_modT(l, 5, j, BC[0])
                row_bcast(l, 128 + 2 * 1024, 1024, BC[1].ap(), [BC[1]])
                tt("dve", BC[1].ap(), BC[1].ap(), BC[0].ap(), ALU.mult, r=[BC[1], BC[0]], w=[BC[1]])
            if not last:
                prep_B(1)
                stage_B(l, NSL, x1D, x2D, 1)
            prep_B(0)
            for fp in full_pairs:
                stage_B(l, fp, x1D, x2D, 0, out_rows=((fp - 2) * 128 if last else None))
            if not last:
                load_phaseA_weights(l + 1)
            xcur = x2D
        S.emit(nc, st)
    build_program.last_sched = S
    return nc

SPLIT_POOL = 256
SPLIT_Q = 768
SPLIT_K = 896
SPLIT_V = 1024
SPLIT_CONV = 1536


def _host_prep(inp):
    f32 = np.float32
    w_in = np.asarray(inp["w_in"], f32)
    b_in = np.asarray(inp["b_in"], f32)
    cols = []
    cols += list(range(SPLIT_Q, SPLIT_K))
    cols += list(range(SPLIT_K, SPLIT_V))
    cols += list(range(0, 256))
    cols += list(range(SPLIT_V, SPLIT_V + 512))
    for j in range(4):
        for h in range(2):
            hd = h * 4 + j
            cols += list(range(SPLIT_POOL + hd * 64, SPLIT_POOL + hd * 64 + 64))
    for c in range(8):
        for b in range(3):
            cols += list(range(SPLIT_CONV + b * 1024 + c * 128, SPLIT_CONV + b * 1024 + c * 128 + 128))
    cols = np.asarray(cols)
    assert cols.shape[0] == 36 * 128
    w_in_r = np.ascontiguousarray(w_in[:, :, cols])
    b_in_r = b_in[:, cols]
    fm = np.zeros((DEPTH, 128, NFM), f32)
    rowv = np.zeros((DEPTH, 1, NROW), f32)
    pbd = np.zeros((DEPTH, 128, 256), f32)
    wao_r = np.zeros((DEPTH, 128, 4 * D), f32)
    for l in range(DEPTH):
        fm[l, :, FM_BIN:FM_BIN + 36] = b_in_r[l].reshape(36, 128).T
        fm[l, :, FM_B1:FM_B1 + 32] = np.asarray(inp["b_mlp1"], f32)[l].reshape(32, 128).T
        fm[l, :, FM_BMOD:FM_BMOD + 48] = np.asarray(inp["b_mod"], f32)[l].reshape(48, 128).T
        fm[l, :, FM_PS:FM_PS + 2] = np.asarray(inp["pool_scale"], f32)[l].reshape(2, 128).T
        fm[l, :, FM_CB:FM_CB + 2] = np.asarray(inp["conv_b"], f32)[l].reshape(2, 128).T
        fm[l, :, FM_CG:FM_CG + 2] = np.asarray(inp["conv_ln_g"], f32)[l].reshape(2, 128).T
        fm[l, :, FM_CBETA:FM_CBETA + 2] = np.asarray(inp["conv_ln_b"], f32)[l].reshape(2, 128).T
        cw = np.asarray(inp["conv_w"], f32)[l]
        for c in range(2):
            fm[l, :, FM_CW + c * 31:FM_CW + (c + 1) * 31] = cw[:, c * 128:(c + 1) * 128].T
        sink = np.asarray(inp["attn_sink"], f32)[l]
        for h in range(2):
            fm[l, h * 64:(h + 1) * 64, FM_SINK:FM_SINK + 4] = sink[h * 4:(h + 1) * 4][None, :]
        rowv[l, 0, 0:128] = b_in[l, SPLIT_K:SPLIT_V]
        rowv[l, 0, 128:1152] = np.asarray(inp["ln1_g"], f32)[l]
        rowv[l, 0, 1152:2176] = np.asarray(inp["ln1_b"], f32)[l]
        rowv[l, 0, 2176:3200] = np.asarray(inp["b_mlp2"], f32)[l]
        rowv[l, 0, 3200:4224] = np.asarray(inp["ln2_g"], f32)[l]
        rowv[l, 0, 4224:5248] = np.asarray(inp["ln2_b"], f32)[l]
        pw = np.asarray(inp["pool_w"], f32)[l]
        for c in range(2):
            for gl in range(2):
                pbd[l, gl * 64:(gl + 1) * 64, c * 128 + gl * 64:c * 128 + (gl + 1) * 64] = pw[2 * c + gl]
        wao = np.asarray(inp["w_attn_out"], f32)[l]
        for h in range(2):
            for g in range(4):
                wao_r[l, h * 64:(h + 1) * 64, g * D:(g + 1) * D] = wao[(h * 4 + g) * 64:(h * 4 + g + 1) * 64, :]
    consts = np.zeros((128, 4, 128), f32)
    consts[:, 0, :] = np.eye(128, dtype=f32)
    p = np.arange(128)
    partner = np.where((p % 32) < 16, p + 16, p - 16)
    consts[partner, 1, p] = 1.0
    kk = np.arange(128)[:, None]
    qq = np.arange(128)[None, :]
    consts[:, 2, :] = (kk >= qq).astype(f32)
    consts[:, 3, :] = (kk <= qq).astype(f32)
    shared = dict(
        consts=consts, fm=fm, rowv=rowv, w_mod=np.ascontiguousarray(np.asarray(inp["w_mod"], f32)),
        w_in_r=w_in_r, pbd=pbd, w_pool_out=np.ascontiguousarray(np.asarray(inp["w_pool_out"], f32)),
        wao_r=wao_r, w_conv_out=np.ascontiguousarray(np.asarray(inp["w_conv_out"], f32)),
        w_out=np.ascontiguousarray(np.asarray(inp["w_out"], f32)),
        w_mlp1=np.ascontiguousarray(np.asarray(inp["w_mlp1"], f32)),
        w_mlp2=np.ascontiguousarray(np.asarray(inp["w_mlp2"], f32)),
    )
    return shared


def _core_tables(a):
    f32 = np.float32
    pos = a - 256 + np.arange(NSL * 128)
    valid = (pos >= 0) & (pos < L)
    tab = np.zeros((128, 5, NT * 128), f32)
    p = np.arange(128)
    d = p % 64
    inv = (10000.0 ** (-(np.arange(16, dtype=f32)) / 16.0)).astype(f32)
    pc = np.clip(pos, 0, L - 1)
    rowp = (pc // 64).astype(f32)
    colp = (pc % 64).astype(f32)
    posm = np.where((d < 32)[:, None], rowp[None, :], colp[None, :]).astype(f32)
    ang = (posm * inv[d % 16][:, None]).astype(f32)
    sgn = np.where((d % 32) < 16, -1.0, 1.0).astype(f32)
    tab[:, 0, :NSL * 128] = np.cos(ang)
    tab[:, 1, :NSL * 128] = np.sin(ang) * sgn[:, None]
    tab[:, 0, NSL * 128:] = 1.0
    tab[:, 2, :NSL * 128] = valid.astype(f32)[None, :]
    tab[:, 2, NSL * 128:] = 1.0

    def pinv(t, w, Ls):
        lo = np.clip(t - w // 2, 0, Ls)
        hi = np.clip(t + w - w // 2, 0, Ls)
        cnt = (hi - lo).astype(f32)
        return np.where(cnt > 0, 1.0 / np.maximum(cnt, 1.0), 0.0).astype(f32)
    tc = np.arange(CTX)
    for ci, (wlo, whi) in enumerate(((2, 4), (8, 16))):
        tab[0:64, 3 + ci, :NSL * 128] = pinv(pos, wlo, L)[None, :]
        tab[64:128, 3 + ci, :NSL * 128] = pinv(pos, whi, L)[None, :]
        tab[0:64, 3 + ci, NSL * 128:] = pinv(tc, wlo, CTX)[None, :]
        tab[64:128, 3 + ci, NSL * 128:] = pinv(tc, whi, CTX)[None, :]
    kb = np.zeros((128, 32), f32)
    v2 = valid.reshape(NSL, 128).T
    kb[:, :NSL] = np.where(v2, 0.0, -30000.0)
    return tab, kb


_NC_CACHE = {}


def kernel(**inp):
    f32 = np.float32
    x = np.asarray(inp["x"], f32)
    c = np.asarray(inp["c"], f32)
    ctx = np.asarray(inp["ctx"], f32)
    c_ctx = np.asarray(inp["c_ctx"], f32)
    B = x.shape[0]
    shared = _host_prep(inp)
    xpad = np.zeros((B, L + 512, D), f32)
    xpad[:, 256:256 + L] = x
    in_maps = []
    for r in range(8):
        b, q = r // 4, r % 4
        a = q * 2048
        xin = np.concatenate([xpad[b, a:a + NSL * 128], ctx[b]], axis=0)
        tab, kb = _core_tables(a)
        cvec = np.zeros((128, 16), f32)
        cvec[:, 0::2] = c[b].reshape(8, 128).T
        cvec[:, 1::2] = c_ctx.reshape(8, 128).T
        m = dict(shared)
        m.update(xin=np.ascontiguousarray(xin), tab=tab, kbias=kb, cvec=cvec)
        in_maps.append(m)
    if "nc" not in _NC_CACHE:
        _NC_CACHE["nc"] = build_program()
    nc = _NC_CACHE["nc"]
    res = run_bass_kernel_spmd(nc, in_maps, core_ids=list(range(8)))
    out = np.zeros((B, L, D), f32)
    for r in range(8):
        b, q = r // 4, r % 4
        out[b, q * 2048:(q + 1) * 2048] = np.asarray(res.results[r]["out"], f32)
    return out
```
